# Optimizing a Trainium2 kernel written in Bass

```python
import jax, jax.numpy as jnp
from jax import lax
import numpy as np

D_MODEL = 1024
BATCH = 16
SEQ = 2048
DEPTH = 2
DEC_BATCH = 16
DEC_SEQ = 64
PAST_LEN = 1024

CHUNK = 64
D_POOL = D_MODEL // 2
N_POOL_GROUPS = 4
POOL_GROUP = D_POOL // N_POOL_GROUPS
POOL_WINDOWS = (2, 4, 8, 16)
POOL_BUF = max(POOL_WINDOWS) - 1
D_LSTM = D_MODEL - D_POOL
N_HEADS = 4
HEAD_DIM = D_LSTM // N_HEADS
D_MIX = D_POOL + D_LSTM
D_IN = D_POOL + 4 * D_LSTM + 2 * N_HEADS
D_FF = 2816
CONV_W = 3
CONV_BUF = CONV_W - 1
EPS = 1e-6

kernel_name = 'hymba_pool_mlstm_convffn_stream'


def rmsnorm(x, g):
    xf = x.astype(jnp.float32)
    y = xf * lax.rsqrt(jnp.mean(xf * xf, -1, keepdims=True) + EPS)
    return y * g.astype(jnp.float32)


def pool_mixer(u, buf, start, w_pool, s_pool):
    B, T, _ = u.shape
    uf = u.astype(jnp.float32)
    up = jnp.concatenate([buf.astype(jnp.float32), uf], 1)
    csum = jnp.concatenate([jnp.zeros_like(up[:, :1]), jnp.cumsum(up, 1)], 1)
    end = csum[:, POOL_BUF + 1:]
    pos = start + jnp.arange(T)
    outs = []
    for g, w in enumerate(POOL_WINDOWS):
        sl = slice(g * POOL_GROUP, (g + 1) * POOL_GROUP)
        begin = csum[:, POOL_BUF + 1 - w: POOL_BUF + 1 - w + T, sl]
        cnt = jnp.minimum(w, pos + 1).astype(jnp.float32)[None, :, None]
        outs.append((end[..., sl] - begin) / cnt)
    pooled = jnp.stack(outs, 2)
    diff = pooled - uf.reshape(B, T, N_POOL_GROUPS, POOL_GROUP)
    y = jnp.einsum('btgc,gcd->btgd', diff, w_pool.astype(jnp.float32)).reshape(B, T, D_POOL)
    return y * s_pool.astype(jnp.float32), up[:, -POOL_BUF:]


def mlstm_chunk(carry, xs):
    C, n, m = carry
    q, k, v, ig, lf = xs
    L = q.shape[2]
    b = jnp.cumsum(lf, -1)
    dmat = b[..., :, None] - b[..., None, :] + ig[..., None, :]
    causal = jnp.tril(jnp.ones((L, L), dtype=bool))
    dmat = jnp.where(causal, dmat, -jnp.inf)
    inter = b + m[..., None]
    m_t = jnp.maximum(inter, dmat.max(-1))
    wt = jnp.exp(dmat - m_t[..., None])
    g_inter = jnp.exp(inter - m_t)
    a = wt * jnp.einsum('bhtd,bhsd->bhts', q, k)
    num = jnp.einsum('bhts,bhsd->bhtd', a, v) + g_inter[..., None] * jnp.einsum('bhvk,bhtk->bhtv', C, q)
    den = a.sum(-1) + g_inter * jnp.einsum('bhk,bhtk->bht', n, q)
    h = num / jnp.maximum(jnp.abs(den), jnp.exp(-m_t))[..., None]
    w_last = wt[..., -1, :]
    g_last = g_inter[..., -1]
    C_new = g_last[..., None, None] * C + jnp.einsum('bhs,bhsv,bhsk->bhvk', w_last, v, k)
    n_new = g_last[..., None] * n + jnp.einsum('bhs,bhsk->bhk', w_last, k)
    return (C_new, n_new, m_t[..., -1]), h


def mlstm_seq(q, k, v, ig, lf, C, n, m):
    B, T, H, Dh = q.shape
    L = CHUNK if T % CHUNK == 0 else T
    nc = T // L

    def to_chunks(a):
        a = a.astype(jnp.float32).reshape((B, nc, L) + a.shape[2:])
        return jnp.moveaxis(jnp.moveaxis(a, 1, 0), 2, 3)

    xs = (to_chunks(q), to_chunks(k), to_chunks(v), to_chunks(ig), to_chunks(lf))
    carry = (C.astype(jnp.float32), n.astype(jnp.float32), m.astype(jnp.float32))
    (C, n, m), h = lax.scan(mlstm_chunk, carry, xs)
    h = jnp.moveaxis(jnp.moveaxis(h, 3, 2), 0, 1).reshape(B, T, H, Dh)
    return h, C, n, m


def layer(x, c_act, pool_buf, C, n, m, conv_buf, start,
          w_ada, b_ada, g1, w_in, b_gate, w_pool, s_pool, g_head, w_out,
          g2, w_up, w_conv, b_conv, w_down):
    B, T, _ = x.shape
    mod = (c_act @ w_ada + b_ada)[:, None, :]
    sh1, sc1, gt1, sh2, sc2, gt2 = jnp.split(mod, 6, -1)
    h = rmsnorm(x, g1) * (1 + sc1) + sh1
    proj = h @ w_in
    cuts = [D_POOL, D_POOL + D_LSTM, D_POOL + 2 * D_LSTM, D_POOL + 3 * D_LSTM, D_POOL + 4 * D_LSTM]
    u, q, k, v, o, gates = jnp.split(proj, cuts, -1)
    gates = gates.astype(jnp.float32) + b_gate.astype(jnp.float32)
    ig = gates[..., :N_HEADS]
    lf = jax.nn.log_sigmoid(gates[..., N_HEADS:])
    pool_out, pool_buf_new = pool_mixer(u, pool_buf, start, w_pool, s_pool)
    hd = (B, T, N_HEADS, HEAD_DIM)
    hs, C, n, m = mlstm_seq(q.reshape(hd), k.reshape(hd) * (HEAD_DIM ** -0.5), v.reshape(hd), ig, lf, C, n, m)
    hs = hs * lax.rsqrt(jnp.mean(hs * hs, -1, keepdims=True) + EPS) * g_head.reshape(N_HEADS, HEAD_DIM)
    hs = hs.reshape(B, T, D_LSTM) * jax.nn.sigmoid(o.astype(jnp.float32))
    mix = jnp.concatenate([pool_out, hs], -1) @ w_out
    x = x + gt1 * mix
    h2 = rmsnorm(x, g2) * (1 + sc2) + sh2
    up = h2 @ w_up
    upp = jnp.concatenate([conv_buf.astype(up.dtype), up], 1)
    conv = b_conv + sum(w_conv[j] * upp[:, j:j + T] for j in range(CONV_W))
    a, g = jnp.split(conv, 2, -1)
    x = x + gt2 * ((jax.nn.silu(g) * a) @ w_down)
    return x, pool_buf_new, C, n, m, upp[:, -CONV_BUF:]


def setup_inputs(seed: int = 0) -> dict:
    key = jax.random.key(seed)
    ks = jax.random.split(key, 32)
    f32 = jnp.float32
    nrm = lambda k, s, sc: jax.random.normal(k, s, f32) * sc
    b_i = nrm(ks[0], (DEPTH, N_HEADS), 0.1)
    b_f = jnp.linspace(3.0, 6.0, N_HEADS, dtype=f32)[None, :] + nrm(ks[1], (DEPTH, N_HEADS), 0.1)
    return {
        'x_prompt': nrm(ks[2], (BATCH, SEQ, D_MODEL), 1.0),
        'x_sample': nrm(ks[3], (DEC_BATCH, DEC_SEQ, D_MODEL), 1.0),
        'state_pool': nrm(ks[4], (DEPTH, DEC_BATCH, POOL_BUF, D_POOL), 1.0),
        'state_mlstm_C': nrm(ks[5], (DEPTH, DEC_BATCH, N_HEADS, HEAD_DIM, HEAD_DIM), 0.1),
        'state_mlstm_n': nrm(ks[6], (DEPTH, DEC_BATCH, N_HEADS, HEAD_DIM), 0.1),
        'state_mlstm_m': jax.random.uniform(ks[7], (DEPTH, DEC_BATCH, N_HEADS), f32, 0.0, 1.0),
        'state_conv': nrm(ks[8], (DEPTH, DEC_BATCH, CONV_BUF, 2 * D_FF), 1.0),
        'c_prompt': nrm(ks[9], (BATCH, D_MODEL), 1.0),
        'c_sample': nrm(ks[10], (DEC_BATCH, D_MODEL), 1.0),
        'w_ada': nrm(ks[11], (DEPTH, D_MODEL, 6 * D_MODEL), 0.5 * D_MODEL ** -0.5),
        'b_ada': nrm(ks[12], (DEPTH, 6 * D_MODEL), 0.01),
        'g_norm1': 1.0 + nrm(ks[13], (DEPTH, D_MODEL), 0.05),
        'w_in': nrm(ks[14], (DEPTH, D_MODEL, D_IN), D_MODEL ** -0.5),
        'b_gate': jnp.concatenate([b_i, b_f], -1),
        'w_pool': nrm(ks[15], (DEPTH, N_POOL_GROUPS, POOL_GROUP, POOL_GROUP), POOL_GROUP ** -0.5),
        's_pool': 1.0 + nrm(ks[16], (DEPTH, D_POOL), 0.05),
        'g_head': 1.0 + nrm(ks[17], (DEPTH, D_LSTM), 0.05),
        'w_out': nrm(ks[18], (DEPTH, D_MIX, D_MODEL), D_MIX ** -0.5),
        'g_norm2': 1.0 + nrm(ks[19], (DEPTH, D_MODEL), 0.05),
        'w_up': nrm(ks[20], (DEPTH, D_MODEL, 2 * D_FF), D_MODEL ** -0.5),
        'w_conv': nrm(ks[21], (DEPTH, CONV_W, 2 * D_FF), CONV_W ** -0.5),
        'b_conv': nrm(ks[22], (DEPTH, 2 * D_FF), 0.01),
        'w_down': nrm(ks[23], (DEPTH, D_FF, D_MODEL), D_FF ** -0.5),
        'g_final': 1.0 + nrm(ks[24], (D_MODEL,), 0.05),
    }


def reference(x_prompt, x_sample, state_pool, state_mlstm_C, state_mlstm_n, state_mlstm_m, state_conv,
              c_prompt, c_sample, w_ada, b_ada, g_norm1, w_in, b_gate, w_pool, s_pool, g_head, w_out,
              g_norm2, w_up, w_conv, b_conv, w_down, g_final):
    f32 = jnp.float32
    cp = jax.nn.silu(c_prompt)
    cs = jax.nn.silu(c_sample)
    xp, xs = x_prompt, x_sample
    Bp = x_prompt.shape[0]
    pp, Cp, npl, mp, vp = [], [], [], [], []
    ps, Cs, nsl, ms, vs = [], [], [], [], []
    for l in range(DEPTH):
        prm = (w_ada[l], b_ada[l], g_norm1[l], w_in[l], b_gate[l], w_pool[l], s_pool[l], g_head[l],
               w_out[l], g_norm2[l], w_up[l], w_conv[l], b_conv[l], w_down[l])
        xp, a, b, c, d, e = layer(
            xp, cp,
            jnp.zeros((Bp, POOL_BUF, D_POOL), f32),
            jnp.zeros((Bp, N_HEADS, HEAD_DIM, HEAD_DIM), f32),
            jnp.zeros((Bp, N_HEADS, HEAD_DIM), f32),
            jnp.zeros((Bp, N_HEADS), f32),
            jnp.zeros((Bp, CONV_BUF, 2 * D_FF), f32),
            0, *prm)
        pp.append(a); Cp.append(b); npl.append(c); mp.append(d); vp.append(e)
        xs, a, b, c, d, e = layer(
            xs, cs, state_pool[l], state_mlstm_C[l], state_mlstm_n[l], state_mlstm_m[l], state_conv[l],
            PAST_LEN, *prm)
        ps.append(a); Cs.append(b); nsl.append(c); ms.append(d); vs.append(e)
    y_prompt = rmsnorm(xp, g_final)
    y_sample = rmsnorm(xs, g_final)
    return (y_prompt, y_sample,
            jnp.stack(pp), jnp.stack(Cp), jnp.stack(npl), jnp.stack(mp), jnp.stack(vp),
            jnp.stack(ps), jnp.stack(Cs), jnp.stack(nsl), jnp.stack(ms), jnp.stack(vs))
```

```python
import numpy as np
from contextlib import ExitStack
import concourse.bass as bass
import concourse.mybir as mybir
from concourse.bass_utils import run_bass_kernel_spmd

F32 = mybir.dt.float32
BF16 = mybir.dt.bfloat16
AF = mybir.ActivationFunctionType
ALU = mybir.AluOpType

D = 1024
KC = 8
DP = 512
NH = 4
HD = 128
DIN = 2568
DFF = 2816
NFT = 44
NKT = 22
EPS = 1e-6
PVL = 248
NPV = 2 * PVL + 8
WINDOWS = (2, 4, 8, 16)


class View:
    def __init__(self, buf, ap):
        self.buf = buf
        self.ap = ap

    def __getitem__(self, idx):
        return View(self.buf, self.ap[idx])

    def rearrange(self, s, **kw):
        return View(self.buf, self.ap.rearrange(s, **kw))

    def bcast(self, shape):
        return View(self.buf, self.ap.broadcast_to(list(shape)))

    def bitcast(self, dt):
        return View(self.buf, self.ap.bitcast(dt))

    def also(self, *bufs):
        v = View(self.buf, self.ap)
        v.bufs = [self.buf] + [b.buf for b in bufs]
        return v


class Buf(View):
    def __init__(self, name, ap):
        self.name = name
        self.buf = self
        self.ap = ap
        self.last_write = None
        self.readers = {}


WRITE_KEYS = ("out", "accum_out", "ap")
COMPUTE = ("pe", "act", "dve", "pool")
ALLENG = ("pe", "act", "dve", "pool", "sp")


class Sched:
    def __init__(self, nc, es):
        self.nc = nc
        self.es = es
        self.plan = {e: [] for e in ALLENG}
        self.count = {e: 0 for e in ALLENG}
        self.sem = {e: es.enter_context(nc.semaphore("S_" + e)) for e in ALLENG}
        self.seen = {e: {} for e in ALLENG}
        self.dma_sems = {}
        self.dma_cnt = {}
        self.dma_rr = {}
        for q, n in (("sp", 12), ("pool", 12)):
            self.dma_sems[q] = [es.enter_context(nc.semaphore(f"D_{q}{i}")) for i in range(n)]
            self.dma_cnt[q] = [0] * n
            self.dma_rr[q] = 0

    def _wait(self, eng, tick):
        if tick is None:
            return
        kind, key, val = tick
        if kind == "eng" and key == eng and eng == "pe":
            return
        k = (kind, key)
        if self.seen[eng].get(k, 0) >= val:
            return
        self.seen[eng][k] = val
        sem = self.sem[key] if kind == "eng" else self.dma_sems[key[0]][key[1]]
        self.plan[eng].append(("wait", sem, val))

    def _deps(self, eng, reads, writes):
        for b in reads:
            self._wait(eng, b.last_write)
            if getattr(b, "excl", False):
                for (kind, key), t in b.readers.items():
                    if not (kind == "eng" and key == eng):
                        self._wait(eng, t)
        for b in writes:
            self._wait(eng, b.last_write)
            for t in b.readers.values():
                self._wait(eng, t)

    def _commit(self, tick, reads, writes):
        for b in writes:
            b.last_write = tick
            b.readers = {}
        for b in reads:
            if b in writes:
                continue
            b.readers[(tick[0], tick[1])] = tick

    def op(self, eng, method, signal=True, **kw):
        reads, writes, real = [], [], {}
        for k, v in kw.items():
            if isinstance(v, View):
                for bb in getattr(v, "bufs", None) or [v.buf]:
                    (writes if k in WRITE_KEYS else reads).append(bb)
                real[k] = v.ap
            else:
                real[k] = v
        self._deps(eng, reads, writes)
        if signal:
            self.count[eng] += 1
            tick = ("eng", eng, self.count[eng])
        else:
            tick = ("eng", eng, self.count[eng] + 1)
        assert self.count[eng] < 60000
        self.plan[eng].append(("inst", method, real, signal))
        self._commit(tick, reads, writes)

    def dma(self, q, out, in_):
        reads, writes = [], []
        o_ap, i_ap = out, in_
        if isinstance(out, View):
            writes.append(out.buf)
            o_ap = out.ap
        if isinstance(in_, View):
            reads.append(in_.buf)
            i_ap = in_.ap
        self._deps(q, reads, writes)
        n = len(self.dma_sems[q])
        i = self.dma_rr[q]
        self.dma_rr[q] = (i + 1) % n
        self._wait(q, ("dma", (q, i), self.dma_cnt[q][i]))
        self.dma_cnt[q][i] += 16
        tick = ("dma", (q, i), self.dma_cnt[q][i])
        self.plan[q].append(("dma", o_ap, i_ap, self.dma_sems[q][i]))
        self._commit(tick, reads, writes)

    def barrier(self):
        for q in ("sp", "pool"):
            for i in range(len(self.dma_sems[q])):
                if self.dma_cnt[q][i] > 0:
                    self._wait(q, ("dma", (q, i), self.dma_cnt[q][i]))
        for q in ("sp",):
            self.count[q] += 1
            self.plan[q].append(("seminc", self.sem[q]))
        self.op("pool", "memset", ap=self.tick_buf, constant=0.0)
        snap = dict(self.count)
        for e in ALLENG:
            for e2 in ALLENG:
                if e2 != e and snap[e2] > 0:
                    self._wait(e, ("eng", e2, snap[e2]))

    def replay(self, block):
        plan = self.plan

        def run(e, items):
            for it in items:
                if it[0] == "wait":
                    e.wait_ge(it[1], it[2])
                elif it[0] == "inst":
                    ins = getattr(e, it[1])(**it[2])
                    if it[3]:
                        ins.then_inc(self.sem[self._name_of(e)], 1)
                elif it[0] == "dma":
                    e.dma_start(out=it[1], in_=it[2]).then_inc(it[3], 16)
                elif it[0] == "seminc":
                    e.sem_inc(it[1], 1)

        self._names = {}

        @block.tensor
        def _(e):
            self._names[id(e)] = "pe"
            run(e, plan["pe"])

        @block.scalar
        def _(e):
            self._names[id(e)] = "act"
            run(e, plan["act"])

        @block.vector
        def _(e):
            self._names[id(e)] = "dve"
            run(e, plan["dve"])

        @block.gpsimd
        def _(e):
            self._names[id(e)] = "pool"
            run(e, plan["pool"])

        @block.sync
        def _(e):
            self._names[id(e)] = "sp"
            run(e, plan["sp"])

    def _name_of(self, e):
        return self._names[id(e)]


def build_program(T, GT, NB, TS=64, depth=2):
    assert T % GT == 0 and GT % NB == 0 and NB % 128 == 0 and NB <= 512
    nc = bass.Bass("TRN2", target_bir_lowering=False)
    es = ExitStack()
    NBLK = GT // NB
    NSEQ = 4
    SB = 2 * TS

    def din(name, shape, dt=F32):
        return nc.dram_tensor(name, list(shape), dt, kind="ExternalInput").ap()

    def dout(name, shape, dt=F32):
        return nc.dram_tensor(name, list(shape), dt, kind="ExternalOutput").ap()

    xp = din("xp", [2, 128, KC, T])
    xs = din("xs", [128, KC, SB])
    cT_d = din("cT", [128, KC, NSEQ])
    w_ada = din("w_ada", [depth, D, 6 * D])
    w_in = din("w_in", [depth, D, DIN])
    w_pool = din("w_pool", [depth, 4, 128, 128])
    w_out = din("w_out", [depth, D, D])
    w_up = din("w_up", [depth, D, 2 * DFF])
    w_down = din("w_down", [depth, DFF, D])
    pv_d = din("pv", [128, NPV])
    bg_d = din("bg", [4, depth * 2])
    ident_d = din("ident", [128, 128])
    mask_d = din("mask", [128, 512])
    sel_d = din("sel", [128, 256])
    rc_d = din("rc", [128, 64])
    sp_hist = din("sp_hist", [128, depth, 2, 4, 15])
    sp_C = din("sp_C", [128, depth, 2, NH, 129])
    sp_mrow = din("sp_mrow", [depth, 2, 4, 1])
    sp_mbc = din("sp_mbc", [128, depth, 2, NH])
    sp_conv = din("sp_conv", [128, depth, 2, NFT, 2])
    yp = dout("yp", [2, 128, KC, T])
    ys = dout("ys", [128, KC, SB])
    o_pool = dout("o_pool", [128, depth, NSEQ, 4, 15])
    o_C = dout("o_C", [128, depth, NSEQ, NH, 129])
    o_m = dout("o_m", [depth, NSEQ, 4, 1])
    o_conv = dout("o_conv", [128, depth, NSEQ, NFT, 2])

    S = Sched(nc, es)
    S.tick_buf = Buf("pooltick", es.enter_context(nc.sbuf_tensor("pooltick", [128, 1], F32))[:])

    def sbt(name, shape, dt=F32):
        t = es.enter_context(nc.sbuf_tensor(name, list(shape), dt))
        return t

    def sbuf(name, shape, dt=F32):
        t = sbt(name, shape, dt)
        return Buf(name, t[:])

    SBLK = NBLK
    xT_t = sbt("xT", [128, KC, GT + SB])
    xT = [[Buf(f"xT{kc}_{b}", xT_t[:, kc, b * NB:(b + 1) * NB]) for b in range(NBLK)] +
          [Buf(f"xT{kc}_s", xT_t[:, kc, GT:GT + SB])] for kc in range(KC)]
    xT_all = [xT[kc][b] for kc in range(KC) for b in range(NBLK)]
    hT_t = [sbt(f"hT{i}", [128, KC, NB], BF16) for i in range(2)]
    hT = [[Buf(f"hT{i}_{kc}", hT_t[i][:, kc, :]) for kc in range(KC)] for i in range(2)]
    hTs_t = sbt("hTs", [128, KC, SB], BF16)
    hT.append([Buf(f"hTs_{kc}", hTs_t[:, kc, :]) for kc in range(KC)])
    sq = hT[1]
    rstd = sbuf("rstd", [128, NB])
    ntmp = [sbuf(f"ntmp{i}", [128, NB]) for i in range(2)]
    pv = sbuf("pvs", [128, NPV])
    ident = sbuf("ident_s", [128, 128])
    identb = sbuf("identb", [128, 128], BF16)
    onesb = sbuf("onesb", [128, 128], BF16)
    mask = sbuf("mask_s", [128, 512], BF16)
    mask_f = None
    sel = sbuf("sel_s", [128, 256])
    rc = sbuf("rc_s", [128, 64])
    bg = sbuf("bg_s", [4, depth * 2])
    nbf = sbuf("nbf", [4, depth])
    ones4c = sbuf("ones4c", [4, 1])
    cTs = sbuf("cTs", [128, KC, NSEQ])
    scT = sbuf("scT", [128, KC, NSEQ], BF16)
    mod = [sbuf(f"mod{l}", [128, 48, NSEQ]) for l in range(depth)]
    A1 = [sbuf(f"A1_{l}", [128, KC, NSEQ]) for l in range(depth)]
    A2 = [sbuf(f"A2_{l}", [128, KC, NSEQ]) for l in range(depth)]
    zero_col = sbuf("zero_col", [128, 1])
    junk = sbuf("junk", [128, 128], BF16)

    class St:
        pass
    states = {}
    for slot in range(3):
        for l in range(depth):
            st = St()
            n = f"{slot}_{l}"
            st.hist = sbuf("hist" + n, [128, 4, 15])
            st.Dm = sbuf("Dm" + n, [128, NH, 129])
            st.Cbf = sbuf("Cbf" + n, [128, NH, 129], BF16)
            st.glast = sbuf("glast" + n, [128, NH])
            st.Eprev = sbuf("Eprev" + n, [128, NH])
            st.Bprev = sbuf("Bprev" + n, [4, 1])
            st.mprev = sbuf("mprev" + n, [4, 1])
            st.cc = sbuf("cc" + n, [128, NFT, 2])
            states[(slot, l)] = st

    class Sm:
        pass
    smalls = []
    for i in range(3):
        sm = Sm()
        sm.dn = sbuf(f"dn_{i}", [128, 4])
        sm.a1 = sbuf(f"a1_{i}", [128, 4])
        sm.scale = sbuf(f"scale_{i}", [128, 4])
        sm.ssq = sbuf(f"ssq_{i}", [128, 4])
        sm.t2 = sbuf(f"t2_{i}", [128, 4])
        sm.comb = sbuf(f"comb_{i}", [128, 4])
        smalls.append(sm)

    NTMAX = max(NB // 128, 2)

    class Bs:
        pass
    bsets = []
    for i in range(2):
        bs = Bs()
        bs.tok = sbuf(f"btok{i}", [128, NTMAX, 12])
        bs.epv = sbuf(f"bepv{i}", [128, NTMAX, 4])
        bs.t1 = sbuf(f"bt1{i}", [128, NTMAX, 24])
        bs.ex = sbuf(f"bex{i}", [128, NTMAX, 24])
        bsets.append(bs)
    blk_ctr = [0]

    used = 208832 - nc.sbuf_bytes_remaining
    ARENA_BYTES = 114 * 1024
    arena = sbt("arena", [128, ARENA_BYTES // 4])

    class Layout:
        def __init__(self):
            self.off = 0

        def alloc(self, name, shape, dt=F32):
            esz = 4 if dt == F32 else 2
            n = int(np.prod(shape[1:])) * esz
            n4 = (n + 3) // 4
            assert self.off + n4 <= ARENA_BYTES // 4, (name, self.off * 4, n)
            ap = arena[:, self.off:self.off + n4]
            if dt != F32:
                ap = ap.bitcast(dt)
            ne = int(np.prod(shape[1:]))
            ap = ap[:, 0:ne]
            if len(shape) == 3:
                ap = ap.rearrange("p (a b) -> p a b", a=shape[1])
            elif len(shape) == 4:
                ap = ap.rearrange("p (a b c) -> p a b c", a=shape[1], b=shape[2])
            self.off += n4
            if shape[0] < 128:
                ap = ap[0:shape[0]]
            return Buf(name, ap)

    L0 = Layout()
    wada_s = [L0.alloc(f"wada{i}", [128, KC, 1024], BF16) for i in range(2)]

    L1 = Layout()
    wi_u = L1.alloc("wi_u", [128, KC, 512], BF16)
    wi_q = L1.alloc("wi_q", [128, KC, 512], BF16)
    wi_k = L1.alloc("wi_k", [128, KC, 512], BF16)
    wi_vo = L1.alloc("wi_vo", [128, KC, 1024], BF16)
    wi_g = L1.alloc("wi_g", [128, KC, 8], BF16)
    wo_s = L1.alloc("wo_s", [128, KC, 1024], BF16)
    wp_s = L1.alloc("wp_s", [128, 4, 128], BF16)
    qT = [L1.alloc(f"qT{h}", [128, NB], BF16) for h in range(NH)]
    kT = [L1.alloc(f"kT{h}", [128, NB], BF16) for h in range(NH)]
    UT = [[L1.alloc(f"UT{s}_{g}", [128, 15 + (NB if s == 0 else TS)]) for g in range(4)] for s in range(2)]
    ptmp = [L1.alloc(f"ptmp{i}", [128, 15 + NB]) for i in range(2)]
    pfix = L1.alloc("pfix", [128, 16])
    diffT = [L1.alloc(f"diffT{g}", [128, NB], BF16) for g in range(4)]
    mixT = [L1.alloc(f"mixT{k}", [128, NB], BF16) for k in range(KC)]
    igs = L1.alloc("igs", [4, NB])
    lfs = L1.alloc("lfs", [4, NB])
    e1s = lfs
    Brow = L1.alloc("Brow", [4, NB])
    Mrow = L1.alloc("Mrow", [4, NB])
    Erow = Brow
    Grow = igs
    vaug = [L1.alloc(f"vaug{i}", [128, NH, 129], BF16) for i in range(3)]
    ktok = [L1.alloc(f"ktok{i}", [128, 512], BF16) for i in range(3)]
    go = [L1.alloc(f"go{i}", [128, 512]) for i in range(3)]
    SmT = [L1.alloc(f"SmT{i}", [128, NH, 128], BF16) for i in range(2)]
    hs = [L1.alloc("hs0", [128, 512], BF16)] * 2

    L2 = Layout()
    actT = [[L2.alloc(f"actT{j}_{b}", [128, NB], BF16) for b in range(NBLK)] + [L2.alloc(f"actT{j}_s", [128, SB], BF16)]
            for j in range(NKT)]
    wup_s = [L2.alloc(f"wup{i}", [128, KC, 512], BF16) for i in range(2)]
    wdn_s = [L2.alloc(f"wdn{i}", [128, NKT, 128], BF16) for i in range(2)]
    acc = [L2.alloc(f"acc{i}", [128, NB]) for i in range(6)]
    sgt = [L2.alloc(f"sg{i}", [128, NB]) for i in range(2)]
    wada2 = [L2.alloc(f"wada2_{i}", [128, KC, 256], BF16) for i in range(2)]
    UPt = [L2.alloc(f"UP{i}", [128, NB + 4]) for i in range(4)]
    UPp = [Buf(f"UPp{i}", UPt[i].ap) for i in range(4)]
    NACC = 6

    L3 = Layout()
    yT = [L3.alloc(f"yT{i}", [128, KC, NB]) for i in range(2)]
    stC = L3.alloc("stC", [128, NH, 129])

    PS = []
    for i in range(8):
        t = es.enter_context(nc.psum_tensor(f"ps{i}", [128, 512], F32))
        PS.append(Buf(f"ps{i}", t[:]))
        PS[-1].excl = True

    def pvcol(l, off, n=1):
        base = l * PVL + off
        return pv[:, base:base + n]

    OFF_BADA, OFF_G1, OFF_G2, OFF_SPOOL, OFF_GHEAD, OFF_WCONV, OFF_BCONV = 0, 48, 56, 64, 68, 72, 204

    fm_rr = [0]

    def fm_bank(banks):
        b = banks[fm_rr[0] % len(banks)]
        fm_rr[0] += 1
        return b

    S.dma("sp", pv, pv_d)
    S.dma("sp", ident, ident_d)
    S.dma("pool", mask, mask_d)
    S.dma("sp", sel, sel_d)
    S.dma("sp", rc, rc_d)
    S.dma("sp", bg, bg_d)
    S.dma("sp", cTs, cT_d)
    S.op("dve", "tensor_copy", out=identb, in_=ident)
    S.op("dve", "memset", ap=onesb, constant=1.0)
    S.op("dve", "memset", ap=ones4c, constant=1.0)
    S.op("dve", "memset", ap=zero_col, constant=0.0)
    for l in range(depth):
        S.op("dve", "tensor_scalar", out=nbf[:, l:l + 1], in0=bg[:, 2 * l + 1:2 * l + 2], scalar1=-1.0,
             scalar2=None, op0=ALU.mult)
    S.op("act", "activation", out=scT, in_=cTs, func=AF.Silu)

    def mod_finish(l):
        S.op("dve", "tensor_tensor", out=mod[l], in0=PS[0][:, 0:192].rearrange("p (a b) -> p a b", b=NSEQ),
             in1=pvcol(l, OFF_BADA, 48).rearrange("p (a b) -> p a b", b=1).bcast([128, 48, NSEQ]), op=ALU.add)
        for (A, goff, scj) in ((A1[l], OFF_G1, 1), (A2[l], OFF_G2, 4)):
            S.op("dve", "tensor_scalar", out=A, in0=mod[l][:, scj * 8:(scj + 1) * 8, :], scalar1=1.0, scalar2=None,
                 op0=ALU.add)
            S.op("dve", "tensor_tensor", out=A, in0=A,
                 in1=pvcol(l, goff, 8).rearrange("p (a b) -> p a b", b=1).bcast([128, 8, NSEQ]), op=ALU.mult)

    def mod_piece(l, piece):
        wv = w_ada[l].rearrange("(kc p) n -> p kc n", p=128)
        wb = wada2[piece % 2]
        S.dma("pool", wb, wv[:, :, piece * 256:(piece + 1) * 256])
        for fcl in range(2):
            fc = piece * 2 + fcl
            for kc in range(KC):
                S.op("pe", "matmul", signal=(kc == KC - 1), out=PS[0][:, fc * 4:fc * 4 + 4],
                     lhsT=wb[:, kc, fcl * 128:(fcl + 1) * 128], rhs=scT[:, kc, :],
                     start=(kc == 0), stop=(kc == KC - 1))

    for l in range(1):
        wv = w_ada[l].rearrange("(kc p) n -> p kc n", p=128)
        for piece in range(6):
            wb = wada_s[piece % 2]
            S.dma("pool", wb, wv[:, :, piece * 1024:(piece + 1) * 1024])
            for fcl in range(8):
                fc = piece * 8 + fcl
                for kc in range(KC):
                    S.op("pe", "matmul", signal=(kc == KC - 1), out=PS[0][:, fc * 4:fc * 4 + 4],
                         lhsT=wb[:, kc, fcl * 128:(fcl + 1) * 128], rhs=scT[:, kc, :],
                         start=(kc == 0), stop=(kc == KC - 1))
        mod_finish(l)
    S.barrier()

    def norm(blk, ncols, segs, scale_fn, bias_fn, outs):
        for kc in range(KC):
            S.op("act", "activation", out=sq[kc][:, 0:ncols], in_=xT[kc][blk][:, 0:ncols], func=AF.Square)
        for kc in range(KC):
            S.op("pe", "matmul", signal=(kc == KC - 1), out=PS[0][:, 0:ncols], lhsT=onesb, rhs=sq[kc][:, 0:ncols],
                 start=(kc == 0), stop=(kc == KC - 1))
        S.op("act", "activation", out=rstd[:, 0:ncols], in_=PS[0][:, 0:ncols], func=AF.Ln, scale=1.0 / D, bias=EPS)
        S.op("act", "activation", out=rstd[:, 0:ncols], in_=rstd[:, 0:ncols], func=AF.Exp, scale=-0.5)
        for kc in range(KC):
            nt = ntmp[kc % 2]
            S.op("dve", "tensor_tensor", out=nt[:, 0:ncols], in0=xT[kc][blk][:, 0:ncols], in1=rstd[:, 0:ncols],
                 op=ALU.mult)
            for (c0, c1, q) in segs:
                S.op("act", "activation", out=outs[kc][:, c0:c1], in_=nt[:, c0:c1], func=AF.Identity,
                     scale=scale_fn(kc, q), bias=bias_fn(kc, q))

    def load_w1(l):
        wv = w_in[l].rearrange("(kc p) n -> p kc n", p=128)
        S.dma("pool", wi_g, wv[:, :, 2560:2568])
        S.dma("pool", wi_k, wv[:, :, 1024:1536])
        S.dma("pool", wi_q, wv[:, :, 512:1024])
        S.dma("pool", wi_u, wv[:, :, 0:512])
        S.dma("pool", wi_vo, wv[:, :, 1536:2560])
        S.dma("pool", wp_s, w_pool[l].rearrange("g c d -> c g d"))
        S.dma("pool", wo_s, w_out[l].rearrange("(kc p) n -> p kc n", p=128))

    def phase1_block(l, blk, ncols, segs, tiles, first_block):
        h = hT[0]
        norm(blk, ncols, [(c0, c1, q) for (c0, c1, q, _) in segs],
             lambda kc, q: A1[l][:, kc, q:q + 1], lambda kc, q: mod[l][:, 0 * 8 + kc, q:q + 1], h)
        for (ps, c0) in ((PS[3], 0), (PS[4], 4)):
            for kc in range(KC):
                S.op("pe", "matmul", signal=(kc == KC - 1), out=ps[0:4, 0:ncols], lhsT=wi_g[:, kc, c0:c0 + 4],
                     rhs=h[kc][:, 0:ncols], start=(kc == 0), stop=(kc == KC - 1))
        S.op("act", "activation", out=igs[:, 0:ncols], in_=PS[3][0:4, 0:ncols], func=AF.Identity,
             bias=bg[:, 2 * l:2 * l + 1], scale=1.0)
        S.op("act", "activation", out=e1s[:, 0:ncols], in_=PS[4][0:4, 0:ncols], func=AF.Exp,
             bias=nbf[:, l:l + 1], scale=-1.0)
        S.op("act", "activation", out=e1s[:, 0:ncols], in_=e1s[:, 0:ncols], func=AF.Ln, bias=1.0, scale=1.0)
        S.op("dve", "tensor_scalar", out=lfs[:, 0:ncols], in0=e1s[:, 0:ncols], scalar1=-1.0, scalar2=None,
             op0=ALU.mult)
        for (c0, c1, q, slot) in segs:
            st = states[(slot, l)]
            S.op("dve", "tensor_tensor_scan", out=Brow[:, c0:c1], data0=ones4c.bcast([4, c1 - c0]), data1=lfs[:, c0:c1],
                 initial=st.Bprev[:, 0:1], op0=ALU.mult, op1=ALU.add)
            S.op("dve", "tensor_tensor_scan", out=Mrow[:, c0:c1], data0=lfs[:, c0:c1], data1=igs[:, c0:c1],
                 initial=st.mprev[:, 0:1], op0=ALU.add, op1=ALU.max)
            S.op("dve", "tensor_copy", out=st.Bprev, in_=Brow[:, c1 - 1:c1])
            S.op("dve", "tensor_copy", out=st.mprev, in_=Mrow[:, c1 - 1:c1])
        S.op("dve", "tensor_tensor", out=Grow[:, 0:ncols], in0=igs[:, 0:ncols], in1=Brow[:, 0:ncols], op=ALU.subtract)
        S.op("dve", "tensor_tensor", out=Erow[:, 0:ncols], in0=Brow[:, 0:ncols], in1=Mrow[:, 0:ncols], op=ALU.subtract)

        def proj_fm(wbuf, m):
            ps = fm_bank([PS[1], PS[2]])
            for kc in range(KC):
                S.op("pe", "matmul", signal=(kc == KC - 1), out=ps[:, 0:ncols], lhsT=wbuf[:, kc, m * 128:(m + 1) * 128],
                     rhs=h[kc][:, 0:ncols], start=(kc == 0), stop=(kc == KC - 1))
            return ps
        for hh in range(NH):
            ps = proj_fm(wi_k, hh)
            S.op("act", "activation", out=kT[hh][:, 0:ncols], in_=ps[:, 0:ncols], func=AF.Copy, scale=float(HD ** -0.5))
        for hh in range(NH):
            ps = proj_fm(wi_q, hh)
            S.op("act", "activation", out=qT[hh][:, 0:ncols], in_=ps[:, 0:ncols], func=AF.Copy)
        for g in range(4):
            ps = proj_fm(wi_u, g)
            for si, (c0, c1, q, slot) in enumerate(segs):
                S.op("act", "activation", out=UT[si][g][:, 15:15 + (c1 - c0)], in_=ps[:, c0:c1], func=AF.Copy)

        for si, (c0, c1, q, slot) in enumerate(segs):
            st = states[(slot, l)]
            n = c1 - c0
            W = 15 + n
            for g in range(4):
                S.op("pool", "tensor_copy", out=UT[si][g][:, 0:15], in_=st.hist[:, g, :])
            for g, w in enumerate(WINDOWS):
                U = UT[si][g]
                nlev = {2: 1, 4: 2, 8: 3, 16: 4}[w]
                starts = [0] * (nlev + 1)
                starts[nlev] = 15
                for j in range(nlev, 1, -1):
                    starts[j - 1] = starts[j] - 2 ** (j - 1)
                src = U
                for j in range(1, nlev + 1):
                    dst = ptmp[j % 2]
                    s0 = starts[j]
                    sh = 2 ** (j - 1)
                    S.op("pool", "tensor_tensor", out=dst[:, s0:W], in0=src[:, s0:W], in1=src[:, s0 - sh:W - sh], op=ALU.add)
                    src = dst
                other = ptmp[(nlev + 1) % 2]
                S.op("pool", "tensor_scalar", out=other[:, 15:W], in0=src[:, 15:W], scalar1=1.0 / w, scalar2=0.0,
                     op0=ALU.mult, op1=ALU.add)
                S.op("pool", "tensor_tensor", out=diffT[g][:, c0:c1], in0=other[:, 15:W], in1=U[:, 15:W], op=ALU.subtract)
                if first_block:
                    S.op("pool", "tensor_tensor", out=pfix, in0=src[:, 15:31], in1=rc[:, 16 * g:16 * g + 16], op=ALU.mult)
                    S.op("pool", "tensor_tensor", out=diffT[g][:, c0:c0 + 16], in0=pfix, in1=U[:, 15:31], op=ALU.subtract)
                S.op("pool", "tensor_copy", out=st.hist[:, g, :], in_=U[:, W - 15:W])

        Pbanks = [(PS[3], PS[4]), (PS[5], PS[6])]
        RB = [PS[0], PS[1], PS[2], PS[7]]
        bs = bsets[blk_ctr[0] % 2]
        blk_ctr[0] += 1
        nt = len(tiles)
        LL = tiles[0][1]
        selL = sel[0:LL, 0:128] if LL == 128 else sel[0:LL, 128:256]
        EB0 = 12 * NTMAX
        for ti, (c0, L, q, slot) in enumerate(tiles):
            for j, row in enumerate((Erow, Grow, Mrow)):
                S.op("pe", "transpose", signal=(ti == nt - 1 and j == 2), out=PS[0][0:L, 12 * ti + 4 * j:12 * ti + 4 * j + 4],
                     in_=row[:, c0:c0 + L], identity=ident[0:4, 0:4])
        S.op("dve", "tensor_copy", out=bs.tok[0:LL, 0:nt, :],
             in_=PS[0][0:LL, 0:12 * nt].rearrange("p (a b) -> p a b", b=12))
        for ti in range(nt):
            S.op("pe", "matmul", signal=(ti == nt - 1), out=PS[0][:, EB0 + 4 * ti:EB0 + 4 * ti + 4], lhsT=selL,
                 rhs=bs.tok[0:LL, ti, 0:4], start=True, stop=True)
        ebc = PS[0][:, EB0:EB0 + 4 * nt].rearrange("p (a b) -> p a b", b=4)
        prev_of = {}
        last_of = {}
        for ti, (c0, L, q, slot) in enumerate(tiles):
            prev_of[ti] = last_of.get(slot)
            last_of[slot] = ti
        ti = 0
        while ti < nt:
            if prev_of[ti] is None:
                S.op("dve", "tensor_copy", out=bs.epv[:, ti, :], in_=states[(tiles[ti][3], l)].Eprev)
                ti += 1
            else:
                t_end = ti
                while t_end < nt and prev_of[t_end] == t_end - 1:
                    t_end += 1
                S.op("dve", "tensor_copy", out=bs.epv[:, ti:t_end, :], in_=ebc[:, ti - 1:t_end - 1, :])
                ti = t_end
        S.op("dve", "tensor_tensor", out=bs.t1[0:LL, 0:nt, 0:4], in0=bs.tok[0:LL, 0:nt, 4:8], in1=bs.epv[0:LL, 0:nt, :],
             op=ALU.add)
        S.op("dve", "tensor_tensor", out=bs.t1[0:LL, 0:nt, 4:8], in0=bs.tok[0:LL, 0:nt, 0:4], in1=bs.epv[0:LL, 0:nt, :],
             op=ALU.subtract)
        S.op("dve", "scalar_tensor_tensor", out=bs.t1[0:LL, 0:nt, 8:12], in0=bs.t1[0:LL, 0:nt, 4:8], scalar=-1.0,
             in1=bs.tok[0:LL, 0:nt, 8:12], op0=ALU.mult, op1=ALU.subtract)
        S.op("dve", "tensor_scalar", out=bs.t1[0:LL, 0:nt, 12:16], in0=bs.t1[0:LL, 0:nt, 8:12], scalar1=2.0, scalar2=None,
             op0=ALU.mult)
        S.op("dve", "tensor_scalar", out=bs.t1[0:LL, 0:nt, 16:20], in0=bs.t1[0:LL, 0:nt, 4:8], scalar1=2.0, scalar2=None,
             op0=ALU.mult)
        S.op("dve", "tensor_tensor", out=bs.t1[:, 0:nt, 20:24], in0=ebc, in1=bs.epv[:, 0:nt, :], op=ALU.subtract)
        S.op("act", "activation", out=bs.ex[0:LL, 0:nt, 0:20], in_=bs.t1[0:LL, 0:nt, 0:20], func=AF.Exp)
        S.op("act", "activation", out=bs.ex[:, 0:nt, 20:24], in_=bs.t1[:, 0:nt, 20:24], func=AF.Exp)
        for slot, ti in last_of.items():
            S.op("dve", "tensor_copy", out=states[(slot, l)].Eprev, in_=ebc[:, ti, :])

        def stageA(ti):
            (c0, L, q, slot) = tiles[ti]
            st = states[(slot, l)]
            sm = smalls[ti % 3]
            p2, p3 = ti % 2, ti % 3
            cs = slice(c0, c0 + L)
            u_ = bs.ex[0:L, ti, 0:4]
            va = vaug[p3]
            psv = fm_bank(RB)
            for kc in range(KC):
                S.op("pe", "matmul", signal=(kc == KC - 1), out=psv[0:L, :], lhsT=h[kc][:, cs], rhs=wi_vo[:, kc, 0:512],
                     start=(kc == 0), stop=(kc == KC - 1))
            S.op("dve", "tensor_tensor", out=va[0:L, :, 0:128], in0=psv[0:L, :].rearrange("p (a b) -> p a b", a=NH),
                 in1=u_.rearrange("p (a b) -> p a b", b=1).bcast([L, NH, 128]), op=ALU.mult)
            S.op("dve", "tensor_copy", out=va[0:L, :, 128:129], in_=u_.rearrange("p (a b) -> p a b", b=1))
            pso = fm_bank(RB)
            for kc in range(KC):
                S.op("pe", "matmul", signal=(kc == KC - 1), out=pso[0:L, :], lhsT=h[kc][:, cs], rhs=wi_vo[:, kc, 512:1024],
                     start=(kc == 0), stop=(kc == KC - 1))
            S.op("act", "activation", out=go[p3][0:L, :], in_=pso[0:L, :], func=AF.Exp, scale=-1.0)
            S.op("act", "activation", out=go[p3][0:L, :], in_=go[p3][0:L, :], func=AF.Ln, bias=1.0, scale=1.0)
            S.op("act", "activation", out=go[p3][0:L, :], in_=go[p3][0:L, :], func=AF.Exp, scale=-1.0)
            pst = fm_bank(RB)
            pstb = pst.bitcast(BF16)
            for hh in range(NH):
                S.op("pe", "transpose", signal=(hh == NH - 1), out=pstb[0:L, hh * 128:(hh + 1) * 128], in_=kT[hh][:, cs],
                     identity=identb)
            S.op("act", "activation", out=ktok[p3][0:L, :], in_=pstb[0:L, 0:512], func=AF.Copy)
            pss = fm_bank(RB)
            for hh in range(NH):
                S.op("pe", "matmul", signal=(hh == NH - 1), out=pss[0:L, hh * 128:hh * 128 + L], lhsT=kT[hh][:, cs],
                     rhs=qT[hh][:, cs], start=True, stop=True)
            S.op("dve", "tensor_tensor", out=SmT[p2][0:L, :, 0:L],
                 in0=pss[0:L, :].rearrange("p (a b) -> p a b", a=NH)[:, :, 0:L],
                 in1=mask[0:L, :].rearrange("p (a b) -> p a b", a=NH)[:, :, 0:L], op=ALU.mult)

        def stageP(ti):
            (c0, L, q, slot) = tiles[ti]
            st = states[(slot, l)]
            sm = smalls[ti % 3]
            p2, p3 = ti % 2, ti % 3
            cs = slice(c0, c0 + L)
            va = vaug[p3]
            PB = Pbanks[ti % 2]
            for hh in range(NH):
                ps = PB[hh // 2]
                o0 = (hh % 2) * 129
                S.op("pe", "matmul", signal=False, out=ps[0:L, o0:o0 + 129], lhsT=SmT[p2][0:L, hh, 0:L],
                     rhs=va[0:L, hh, :], start=True, stop=False)
                S.op("pe", "matmul", signal=True, out=ps[0:L, o0:o0 + 129], lhsT=qT[hh][:, cs], rhs=st.Cbf[:, hh, :],
                     start=False, stop=True)

        def stageQ1(ti):
            (c0, L, q, slot) = tiles[ti]
            sm = smalls[ti % 3]
            PB = Pbanks[ti % 2]
            for hh in range(NH):
                ps = PB[hh // 2]
                o0 = (hh % 2) * 129
                S.op("act", "activation", out=junk[0:L, :], in_=ps[0:L, o0:o0 + 128], func=AF.Square, scale=float(HD ** -0.5),
                     accum_out=sm.ssq[0:L, hh:hh + 1])

        def stageQ(ti):
            (c0, L, q, slot) = tiles[ti]
            st = states[(slot, l)]
            sm = smalls[ti % 3]
            p2, p3 = ti % 2, ti % 3
            cs = slice(c0, c0 + L)
            emg2 = bs.ex[0:L, ti, 12:16]
            PB = Pbanks[ti % 2]
            for pi in range(2):
                S.op("dve", "tensor_copy", out=sm.dn[0:L, 2 * pi:2 * pi + 2],
                     in_=PB[pi][0:L, 0:258].rearrange("p (a b) -> p a b", b=129)[:, :, 128])
            S.op("dve", "tensor_tensor", out=sm.a1[0:L, :], in0=sm.dn[0:L, :], in1=sm.dn[0:L, :], op=ALU.mult)
            S.op("dve", "tensor_tensor", out=sm.t2[0:L, :], in0=sm.a1[0:L, :], in1=emg2, op=ALU.max)
            S.op("dve", "scalar_tensor_tensor", out=sm.t2[0:L, :], in0=sm.t2[0:L, :], scalar=EPS, in1=sm.ssq[0:L, :],
                 op0=ALU.mult, op1=ALU.add)
            S.op("dve", "tensor_tensor", out=sm.t2[0:L, :], in0=sm.t2[0:L, :], in1=bs.ex[0:L, ti, 16:20], op=ALU.mult)
            S.op("act", "activation", out=sm.t2[0:L, :], in_=sm.t2[0:L, :], func=AF.Ln)
            S.op("act", "activation", out=sm.scale[0:L, :], in_=sm.t2[0:L, :], func=AF.Exp, scale=-0.5)
            S.op("dve", "tensor_tensor", out=sm.comb[0:L, :], in0=sm.scale[0:L, :], in1=bs.ex[0:L, ti, 4:8], op=ALU.mult)
            for hh in range(NH):
                ps = PB[hh // 2]
                o0 = (hh % 2) * 129
                S.op("dve", "scalar_tensor_tensor", out=hs[p2][0:L, hh * 128:(hh + 1) * 128], in0=ps[0:L, o0:o0 + 128],
                     scalar=sm.comb[0:L, hh:hh + 1], in1=go[p3][0:L, hh * 128:(hh + 1) * 128], op0=ALU.mult, op1=ALU.mult)
            p7b = fm_bank(RB).bitcast(BF16)
            for hh in range(NH):
                S.op("pe", "transpose", signal=(hh == NH - 1), out=p7b[:, hh * 128:hh * 128 + L],
                     in_=hs[p2][0:L, hh * 128:(hh + 1) * 128], identity=identb[0:L, 0:L])
            for hh in range(NH):
                S.op("act", "activation", out=mixT[4 + hh][:, cs], in_=p7b[:, hh * 128:hh * 128 + L], func=AF.Identity,
                     scale=pvcol(l, OFF_GHEAD + hh))

        def stageS(ti):
            (c0, L, q, slot) = tiles[ti]
            st = states[(slot, l)]
            p2, p3 = ti % 2, ti % 3
            va = vaug[p3]
            SB2 = [fm_bank(RB), fm_bank(RB)]
            for hh in range(NH):
                ps = SB2[hh // 2]
                o0 = (hh % 2) * 129
                S.op("pe", "matmul", out=ps[:, o0:o0 + 129], lhsT=ktok[p3][0:L, hh * 128:(hh + 1) * 128],
                     rhs=va[0:L, hh, :], start=True, stop=True)
            gl = bs.ex[:, ti, 20:24].rearrange("p (a b) -> p a b", b=1).bcast([128, NH, 129])
            for pi in range(2):
                S.op("dve", "tensor_tensor", out=st.Dm[:, 2 * pi:2 * pi + 2, :], in0=st.Dm[:, 2 * pi:2 * pi + 2, :],
                     in1=SB2[pi][:, 0:258].rearrange("p (a b) -> p a b", b=129), op=ALU.add)
            S.op("dve", "tensor_tensor", out=st.Cbf, in0=st.Dm, in1=gl, op=ALU.mult)
            S.op("dve", "tensor_tensor", out=st.Dm, in0=st.Dm, in1=gl, op=ALU.mult)

        stageA(0)
        if nt > 1:
            stageA(1)
        for ti in range(nt):
            stageP(ti)
            stageS(ti)
            stageQ1(ti)
            if ti >= 1:
                stageQ(ti - 1)
            if ti + 2 < nt:
                stageA(ti + 2)
        stageQ(nt - 1)

        for g in range(4):
            ps = fm_bank([PS[1], PS[2]])
            S.op("pe", "matmul", out=ps[:, 0:ncols], lhsT=wp_s[:, g, :], rhs=diffT[g][:, 0:ncols], start=True, stop=True)
            S.op("act", "activation", out=mixT[g][:, 0:ncols], in_=ps[:, 0:ncols], func=AF.Identity,
                 scale=pvcol(l, OFF_SPOOL + g))
        for m in range(KC):
            ps = fm_bank([PS[1], PS[2]])
            for kc in range(KC):
                S.op("pe", "matmul", signal=(kc == KC - 1), out=ps[:, 0:ncols], lhsT=wo_s[:, kc, m * 128:(m + 1) * 128],
                     rhs=mixT[kc][:, 0:ncols], start=(kc == 0), stop=(kc == KC - 1))
            for (c0, c1, q, slot) in segs:
                S.op("dve", "scalar_tensor_tensor", out=xT[m][blk][:, c0:c1], in0=ps[:, c0:c1],
                     scalar=mod[l][:, 2 * 8 + m, q:q + 1], in1=xT[m][blk][:, c0:c1], op0=ALU.mult, op1=ALU.add)

    def phase2(l, blocks, defer_l=None):
        for (blk, ncols, segs) in sorted(blocks, key=lambda b: (b[0] == 1, b[0])):
            norm(blk, ncols, [(c0, c1, q) for (c0, c1, q, _) in segs],
                 lambda kc, q: A2[l][:, kc, q:q + 1], lambda kc, q: mod[l][:, 3 * 8 + kc, q:q + 1], hT[blk])
        banks = [PS[i] for i in range(1, 8)]
        wv = w_up[l].rearrange("(kc p) (h j c) -> p kc h j c", p=128, h=2, c=128)
        acc_rr = 0
        def load_up(jj):
            wbx = wup_s[jj % 2].rearrange("p k (h j c) -> p k h j c", h=2, j=2)
            for half in range(2):
                S.dma("pool", wbx[:, :, half], wv[:, :, half, 2 * jj:2 * jj + 2, :])
        load_up(0)
        wdv = w_down[l].rearrange("(kt p) n -> p kt n", p=128)
        S.dma("pool", wdn_s[0], wdv[:, :, 0:128])
        S.dma("pool", wdn_s[1], wdv[:, :, 128:256])
        pending = []

        def back(item):
            (j, blk, ncols, accs) = item
            sg = sgt[(j * len(blocks) + blk) % 2]
            S.op("act", "activation", out=sg[:, 0:ncols], in_=accs[1][:, 0:ncols], func=AF.Silu)
            S.op("dve", "tensor_tensor", out=actT[j][blk][:, 0:ncols], in0=accs[0][:, 0:ncols], in1=sg[:, 0:ncols],
                 op=ALU.mult)

        dpieces = list(range(24)) if defer_l is not None else []
        for jj in range(NKT // 2):
            wb = wup_s[jj % 2]
            wb4 = wb.rearrange("p k (h j c) -> p k h j c", h=2, j=2)
            if jj + 1 < NKT // 2:
                load_up(jj + 1)
            for _ in range(2):
                if dpieces:
                    mod_piece(defer_l, dpieces.pop(0))
            for jl in range(2):
                j = 2 * jj + jl
                for (blk, ncols, segs) in blocks:
                    accs = []
                    for half in range(2):
                        f = half * NKT + j
                        ps = fm_bank(banks)
                        for kc in range(KC):
                            S.op("pe", "matmul", signal=(kc == KC - 1), out=ps[:, 0:ncols], lhsT=wb4[:, kc, half, jl, :],
                                 rhs=hT[blk][kc][:, 0:ncols], start=(kc == 0), stop=(kc == KC - 1))
                        a = acc[acc_rr % NACC]
                        up = UPt[acc_rr % 4]
                        upp = UPp[acc_rr % 4]
                        acc_rr += 1
                        accs.append(a)
                        w0 = pvcol(l, OFF_WCONV + 0 * NFT + f)
                        w1 = pvcol(l, OFF_WCONV + 1 * NFT + f)
                        w2 = pvcol(l, OFF_WCONV + 2 * NFT + f)
                        bc = pvcol(l, OFF_BCONV + f)
                        S.op("act", "activation", out=a[:, 0:ncols], in_=ps[:, 0:ncols], func=AF.Identity, scale=w2, bias=bc)
                        for si, (c0, c1, q, slot) in enumerate(segs):
                            st = states[(slot, l)]
                            off = c0 + 2 * si
                            n = c1 - c0
                            S.op("dve", "tensor_copy", out=View(upp, up.ap[:, off:off + 2]), in_=st.cc[:, f, :])
                            S.op("act", "activation", out=up[:, off + 2:off + 2 + n], in_=ps[:, c0:c1], func=AF.Copy)
                        for si, (c0, c1, q, slot) in enumerate(segs):
                            st = states[(slot, l)]
                            off = c0 + 2 * si
                            n = c1 - c0
                            S.op("dve", "tensor_copy", out=st.cc[:, f, :], in_=up[:, off + n:off + n + 2])
                            S.op("dve", "scalar_tensor_tensor", out=a[:, c0:c1], in0=up[:, off + 1:off + 1 + n].also(upp),
                                 scalar=w1, in1=a[:, c0:c1], op0=ALU.mult, op1=ALU.add)
                            S.op("dve", "scalar_tensor_tensor", out=a[:, c0:c1], in0=up[:, off:off + n].also(upp), scalar=w0,
                                 in1=a[:, c0:c1], op0=ALU.mult, op1=ALU.add)
                    for it in pending:
                        back(it)
                    pending = [(j, blk, ncols, accs)]
        for it in pending:
            back(it)
        for m in range(KC):
            wb = wdn_s[m % 2]
            while m == 0 and dpieces:
                mod_piece(defer_l, dpieces.pop(0))
            if m == 1 and defer_l is not None:
                mod_finish(defer_l)
            if m >= 1 and m + 1 < KC:
                S.dma("pool", wdn_s[(m + 1) % 2], wdv[:, :, (m + 1) * 128:(m + 2) * 128])
            for (blk, ncols, segs) in blocks:
                ps = fm_bank(banks)
                for kt in range(NKT):
                    S.op("pe", "matmul", signal=(kt == NKT - 1), out=ps[:, 0:ncols], lhsT=wb[:, kt, :],
                         rhs=actT[kt][blk][:, 0:ncols], start=(kt == 0), stop=(kt == NKT - 1))
                for (c0, c1, q, slot) in segs:
                    S.op("dve", "scalar_tensor_tensor", out=xT[m][blk][:, c0:c1], in0=ps[:, c0:c1],
                         scalar=mod[l][:, 5 * 8 + m, q:q + 1], in1=xT[m][blk][:, c0:c1], op0=ALU.mult, op1=ALU.add)

    def init_prompt_states():
        for l in range(depth):
            st = states[(0, l)]
            for b in (st.hist, st.Dm, st.Cbf, st.Eprev, st.cc):
                S.op("dve", "memset", ap=b, constant=0.0)
            S.op("dve", "memset", ap=st.glast, constant=1.0)
            S.op("dve", "memset", ap=st.Bprev, constant=0.0)
            S.op("dve", "memset", ap=st.mprev, constant=0.0)

    def init_sample_states():
        for si in range(2):
            slot = 1 + si
            for l in range(depth):
                st = states[(slot, l)]
                S.dma("sp", st.hist, sp_hist[:, l, si])
                S.dma("sp", st.Dm, sp_C[:, l, si])
                S.dma("sp", st.mprev, sp_mrow[l, si])
                S.dma("sp", st.Eprev, sp_mbc[:, l, si])
                S.dma("sp", st.cc, sp_conv[:, l, si])
                S.op("dve", "tensor_copy", out=st.Cbf, in_=st.Dm)
                S.op("dve", "tensor_scalar", out=st.Eprev, in0=st.Eprev, scalar1=-1.0, scalar2=None, op0=ALU.mult)
                S.op("dve", "memset", ap=st.glast, constant=1.0)
                S.op("dve", "memset", ap=st.Bprev, constant=0.0)

    def store_states(slot, q):
        for l in range(depth):
            st = states[(slot, l)]
            S.dma("sp", o_pool[:, l, q], st.hist)
            S.dma("sp", o_C[:, l, q], st.Dm)
            S.dma("sp", o_m[l, q], st.mprev)
            S.dma("sp", o_conv[:, l, q], st.cc)

    def final_norm_store(blk, ncols, dst_ap, par):
        y = yT[par]
        outs = [Buf_view(y, kc) for kc in range(KC)]
        norm(blk, ncols, [(0, ncols, 0)], lambda kc, q: pv[:, 2 * PVL + kc:2 * PVL + kc + 1],
             lambda kc, q: 0.0, outs)
        S.dma("sp", dst_ap, y[:, :, 0:ncols])

    def Buf_view(y, kc):
        return y[:, kc, :]

    groups = []
    for s in range(2):
        for g in range(T // GT):
            groups.append((s, g))
    sample_segs = [(0, TS, 2, 1), (TS, 2 * TS, 3, 2)]
    sample_tiles = [(0, TS, 2, 1), (TS, TS, 3, 2)]

    for gi, (s, g) in enumerate(groups):
        last = gi == len(groups) - 1
        t0 = g * GT
        S.dma("sp", _multi([xT[kc][b] for kc in range(KC) for b in range(NBLK)], xT_t[:, :, 0:GT]), xp[s][:, :, t0:t0 + GT])
        if g == 0:
            init_prompt_states()
        blocks = [(b, NB, [(0, NB, s, 0)]) for b in range(NBLK)]
        if last:
            S.dma("sp", _multi([xT[kc][SBLK] for kc in range(KC)], xT_t[:, :, GT:GT + SB]), xs)
            init_sample_states()
            blocks.append((SBLK, SB, sample_segs))

        def tiles_of(b):
            return sample_tiles if b == SBLK else [(c, 128, s, 0) for c in range(0, NB, 128)]
        for l in range(depth):
            load_w1(l)
            for (blk, ncols, segs) in blocks:
                phase1_block(l, blk, ncols, segs, tiles_of(blk), first_block=(g == 0 and blk == 0))
            S.barrier()
            phase2(l, blocks, defer_l=(l + 1 if (gi == 0 and l + 1 < depth) else None))
            S.barrier()
        for bi, (blk, ncols, segs) in enumerate(blocks):
            if blk == SBLK:
                dst = ys
            else:
                dst = yp[s][:, :, t0 + blk * NB:t0 + (blk + 1) * NB]
            final_norm_store(blk, ncols, dst, bi % 2)
        if g == T // GT - 1:
            store_states(0, s)
        if last:
            store_states(1, 2)
            store_states(2, 3)
        S.barrier()

    block = es.enter_context(nc.Block())
    S.replay(block)
    es.close()
    return nc


class _MultiView(View):
    pass


def _multi(bufs, ap):
    v = _MultiView(bufs[0], ap)
    v.bufs = bufs
    return v


_orig_dma = Sched.dma


def _dma(self, q, out, in_):
    if isinstance(out, _MultiView):
        extra = out.bufs[1:]
        self._deps(q, [], extra)
        _orig_dma(self, q, View(out.bufs[0], out.ap), in_)
        tick = out.bufs[0].last_write
        for b in extra:
            b.last_write = tick
            b.readers = {}
    else:
        _orig_dma(self, q, out, in_)


Sched.dma = _dma


def _cols(v):
    v = np.asarray(v, np.float32)
    return np.ascontiguousarray(v.reshape(-1, 128).T)


def _consts():
    ident = np.eye(128, dtype=np.float32)
    tri = np.triu(np.ones((128, 128), np.float32))
    mask = np.tile(tri, (1, 4))
    sel = np.zeros((128, 256), np.float32)
    sel[127, 0:128] = 1.0
    sel[63, 128:256] = 1.0
    rc = np.zeros((128, 64), np.float32)
    for g, w in enumerate(WINDOWS):
        for t in range(16):
            rc[:, 16 * g + t] = 1.0 / min(w, t + 1)
    return ident, mask, sel, rc


_PROG_CACHE = {}


def kernel(x_prompt, x_sample, state_pool, state_mlstm_C, state_mlstm_n, state_mlstm_m, state_conv,
           c_prompt, c_sample, w_ada, b_ada, g_norm1, w_in, b_gate, w_pool, s_pool, g_head, w_out,
           g_norm2, w_up, w_conv, b_conv, w_down, g_final, _cfg=None):
    f32 = np.float32
    x_prompt = np.asarray(x_prompt, f32)
    x_sample = np.asarray(x_sample, f32)
    B, T, _ = x_prompt.shape
    BS, TS, _ = x_sample.shape
    depth = w_in.shape[0]
    ncores = B // 2
    assert BS == B
    GT = min(1024, T)
    NB = min(512, GT)
    if _cfg is not None:
        GT, NB = _cfg
    key = (T, GT, NB, TS, depth)
    if key not in _PROG_CACHE:
        _PROG_CACHE[key] = build_program(T, GT, NB, TS, depth)
    nc = _PROG_CACHE[key]

    ident, mask, sel, rc = _consts()
    pvl = []
    for l in range(depth):
        pvl += [_cols(b_ada[l]), _cols(g_norm1[l]), _cols(g_norm2[l]), _cols(s_pool[l]), _cols(g_head[l]),
                np.ascontiguousarray(np.asarray(w_conv[l], f32).reshape(3, NFT, 128).transpose(2, 0, 1).reshape(128, 3 * NFT)),
                _cols(b_conv[l])]
    pvl.append(_cols(g_final))
    pv = np.ascontiguousarray(np.concatenate(pvl, axis=1))
    assert pv.shape == (128, NPV)
    bgv = np.asarray(b_gate, f32)
    bg = np.zeros((4, depth * 2), f32)
    for l in range(depth):
        bg[:, 2 * l] = bgv[l, 0:4]
        bg[:, 2 * l + 1] = bgv[l, 4:8]
    wts = dict(w_ada=np.asarray(w_ada, f32), w_in=np.asarray(w_in, f32), w_pool=np.asarray(w_pool, f32),
               w_out=np.asarray(w_out, f32), w_up=np.asarray(w_up, f32), w_down=np.asarray(w_down, f32))
    state_pool = np.asarray(state_pool, f32)
    state_mlstm_C = np.asarray(state_mlstm_C, f32)
    state_mlstm_n = np.asarray(state_mlstm_n, f32)
    state_mlstm_m = np.asarray(state_mlstm_m, f32)
    state_conv = np.asarray(state_conv, f32)
    c_prompt = np.asarray(c_prompt, f32)
    c_sample = np.asarray(c_sample, f32)

    in_maps = []
    for c in range(ncores):
        bs = slice(2 * c, 2 * c + 2)
        xp = np.ascontiguousarray(x_prompt[bs].reshape(2, T, KC, 128).transpose(0, 3, 2, 1))
        xs = np.ascontiguousarray(x_sample[bs].reshape(2 * TS, KC, 128).transpose(2, 1, 0))
        cc = np.concatenate([c_prompt[bs], c_sample[bs]], 0)
        cT = np.ascontiguousarray(cc.reshape(4, KC, 128).transpose(2, 1, 0))
        hist = np.ascontiguousarray(state_pool[:, bs].reshape(depth, 2, 15, 4, 128).transpose(4, 0, 1, 3, 2))
        Ct = state_mlstm_C[:, bs].transpose(4, 0, 1, 2, 3)
        nt = state_mlstm_n[:, bs].transpose(3, 0, 1, 2)[..., None]
        spC = np.ascontiguousarray(np.concatenate([Ct, nt], axis=-1))
        mrow = np.ascontiguousarray(state_mlstm_m[:, bs][..., None])
        mbc = np.ascontiguousarray(np.broadcast_to(state_mlstm_m[:, bs][None], (128, depth, 2, NH)))
        conv = np.ascontiguousarray(state_conv[:, bs].reshape(depth, 2, 2, NFT, 128).transpose(4, 0, 1, 3, 2))
        m = dict(xp=xp, xs=xs, cT=cT, pv=pv, bg=bg, ident=ident, mask=mask, sel=sel, rc=rc,
                 sp_hist=hist, sp_C=spC, sp_mrow=mrow, sp_mbc=mbc, sp_conv=conv)
        m.update(wts)
        in_maps.append(m)

    res = run_bass_kernel_spmd(nc, in_maps, core_ids=list(range(ncores)))
    R = res.results

    y_prompt = np.zeros((B, T, D), f32)
    y_sample = np.zeros((BS, TS, D), f32)
    outs = {}
    for nm, nb in (("p", B), ("s", BS)):
        outs["pool" + nm] = np.zeros((depth, nb, 15, DP), f32)
        outs["C" + nm] = np.zeros((depth, nb, NH, HD, HD), f32)
        outs["n" + nm] = np.zeros((depth, nb, NH, HD), f32)
        outs["m" + nm] = np.zeros((depth, nb, NH), f32)
        outs["conv" + nm] = np.zeros((depth, nb, 2, 2 * DFF), f32)
    for c in range(ncores):
        r = R[c]
        ypc = np.asarray(r["yp"])
        y_prompt[2 * c:2 * c + 2] = ypc.transpose(0, 3, 2, 1).reshape(2, T, D)
        ysc = np.asarray(r["ys"])
        y_sample[2 * c:2 * c + 2] = ysc.transpose(2, 1, 0).reshape(2, TS, D)
        op_ = np.asarray(r["o_pool"])
        oC = np.asarray(r["o_C"])
        om = np.asarray(r["o_m"])[..., 0]
        oc = np.asarray(r["o_conv"])
        for q in range(4):
            nm = "p" if q < 2 else "s"
            b = 2 * c + (q % 2)
            outs["pool" + nm][:, b] = op_[:, :, q].transpose(1, 3, 2, 0).reshape(depth, 15, DP)
            outs["C" + nm][:, b] = oC[:, :, q, :, 0:128].transpose(1, 2, 3, 0)
            outs["n" + nm][:, b] = oC[:, :, q, :, 128].transpose(1, 2, 0)
            outs["m" + nm][:, b] = om[:, q, :]
            outs["conv" + nm][:, b] = oc[:, :, q].transpose(1, 3, 2, 0).reshape(depth, 2, 2 * DFF)
    return (y_prompt, y_sample,
            outs["poolp"], outs["Cp"], outs["np"], outs["mp"], outs["convp"],
            outs["pools"], outs["Cs"], outs["ns"], outs["ms"], outs["convs"])
```

```python
import numpy as np
from contextlib import ExitStack
import concourse.bass as bass
import concourse.mybir as mybir
from concourse.bass_utils import run_bass_kernel_spmd

F32 = mybir.dt.float32
BF16 = mybir.dt.bfloat16
AF = mybir.ActivationFunctionType
ALU = mybir.AluOpType

D = 1024
KC = 8
DP = 512
NH = 4
HD = 128
DIN = 2568
DFF = 2816
NFT = 44
NKT = 22
EPS = 1e-6
PVL = 248
NPV = 2 * PVL + 8
WINDOWS = (2, 4, 8, 16)


class View:
    def __init__(self, buf, ap):
        self.buf = buf
        self.ap = ap

    def __getitem__(self, idx):
        return View(self.buf, self.ap[idx])

    def rearrange(self, s, **kw):
        return View(self.buf, self.ap.rearrange(s, **kw))

    def bcast(self, shape):
        return View(self.buf, self.ap.broadcast_to(list(shape)))

    def bitcast(self, dt):
        return View(self.buf, self.ap.bitcast(dt))

    def also(self, *bufs):
        v = View(self.buf, self.ap)
        v.bufs = [self.buf] + [b.buf for b in bufs]
        return v


class Buf(View):
    def __init__(self, name, ap):
        self.name = name
        self.buf = self
        self.ap = ap
        self.last_write = None
        self.readers = {}


WRITE_KEYS = ("out", "accum_out", "ap")
COMPUTE = ("pe", "act", "dve", "pool")
ALLENG = ("pe", "act", "dve", "pool", "sp")


class Sched:
    def __init__(self, nc, es):
        self.nc = nc
        self.es = es
        self.plan = {e: [] for e in ALLENG}
        self.count = {e: 0 for e in ALLENG}
        self.sem = {e: es.enter_context(nc.semaphore("S_" + e)) for e in ALLENG}
        self.seen = {e: {} for e in ALLENG}
        self.dma_sems = {}
        self.dma_cnt = {}
        self.dma_rr = {}
        for q, n in (("sp", 12), ("pool", 12)):
            self.dma_sems[q] = [es.enter_context(nc.semaphore(f"D_{q}{i}")) for i in range(n)]
            self.dma_cnt[q] = [0] * n
            self.dma_rr[q] = 0

    def _wait(self, eng, tick):
        if tick is None:
            return
        kind, key, val = tick
        if kind == "eng" and key == eng and eng == "pe":
            return
        k = (kind, key)
        if self.seen[eng].get(k, 0) >= val:
            return
        self.seen[eng][k] = val
        sem = self.sem[key] if kind == "eng" else self.dma_sems[key[0]][key[1]]
        self.plan[eng].append(("wait", sem, val))

    def _deps(self, eng, reads, writes):
        for b in reads:
            self._wait(eng, b.last_write)
            if getattr(b, "excl", False):
                for (kind, key), t in b.readers.items():
                    if not (kind == "eng" and key == eng):
                        self._wait(eng, t)
        for b in writes:
            self._wait(eng, b.last_write)
            for t in b.readers.values():
                self._wait(eng, t)

    def _commit(self, tick, reads, writes):
        for b in writes:
            b.last_write = tick
            b.readers = {}
        for b in reads:
            if b in writes:
                continue
            b.readers[(tick[0], tick[1])] = tick

    def op(self, eng, method, signal=True, **kw):
        reads, writes, real = [], [], {}
        for k, v in kw.items():
            if isinstance(v, View):
                for bb in getattr(v, "bufs", None) or [v.buf]:
                    (writes if k in WRITE_KEYS else reads).append(bb)
                real[k] = v.ap
            else:
                real[k] = v
        self._deps(eng, reads, writes)
        if signal:
            self.count[eng] += 1
            tick = ("eng", eng, self.count[eng])
        else:
            tick = ("eng", eng, self.count[eng] + 1)
        assert self.count[eng] < 60000
        self.plan[eng].append(("inst", method, real, signal))
        self._commit(tick, reads, writes)

    def dma(self, q, out, in_):
        reads, writes = [], []
        o_ap, i_ap = out, in_
        if isinstance(out, View):
            writes.append(out.buf)
            o_ap = out.ap
        if isinstance(in_, View):
            reads.append(in_.buf)
            i_ap = in_.ap
        self._deps(q, reads, writes)
        n = len(self.dma_sems[q])
        i = self.dma_rr[q]
        self.dma_rr[q] = (i + 1) % n
        self._wait(q, ("dma", (q, i), self.dma_cnt[q][i]))
        self.dma_cnt[q][i] += 16
        tick = ("dma", (q, i), self.dma_cnt[q][i])
        self.plan[q].append(("dma", o_ap, i_ap, self.dma_sems[q][i]))
        self._commit(tick, reads, writes)

    def barrier(self):
        for q in ("sp", "pool"):
            for i in range(len(self.dma_sems[q])):
                if self.dma_cnt[q][i] > 0:
                    self._wait(q, ("dma", (q, i), self.dma_cnt[q][i]))
        for q in ("sp",):
            self.count[q] += 1
            self.plan[q].append(("seminc", self.sem[q]))
        self.op("pool", "memset", ap=self.tick_buf, constant=0.0)
        snap = dict(self.count)
        for e in ALLENG:
            for e2 in ALLENG:
                if e2 != e and snap[e2] > 0:
                    self._wait(e, ("eng", e2, snap[e2]))

    def replay(self, block):
        plan = self.plan

        def run(e, items):
            for it in items:
                if it[0] == "wait":
                    e.wait_ge(it[1], it[2])
                elif it[0] == "inst":
                    ins = getattr(e, it[1])(**it[2])
                    if it[3]:
                        ins.then_inc(self.sem[self._name_of(e)], 1)
                elif it[0] == "dma":
                    e.dma_start(out=it[1], in_=it[2]).then_inc(it[3], 16)
                elif it[0] == "seminc":
                    e.sem_inc(it[1], 1)

        self._names = {}

        @block.tensor
        def _(e):
            self._names[id(e)] = "pe"
            run(e, plan["pe"])

        @block.scalar
        def _(e):
            self._names[id(e)] = "act"
            run(e, plan["act"])

        @block.vector
        def _(e):
            self._names[id(e)] = "dve"
            run(e, plan["dve"])

        @block.gpsimd
        def _(e):
            self._names[id(e)] = "pool"
            run(e, plan["pool"])

        @block.sync
        def _(e):
            self._names[id(e)] = "sp"
            run(e, plan["sp"])

    def _name_of(self, e):
        return self._names[id(e)]


def build_program(T, GT, NB, TS=64, depth=2):
    assert T % GT == 0 and GT % NB == 0 and NB % 128 == 0 and NB <= 512
    nc = bass.Bass("TRN2", target_bir_lowering=False)
    es = ExitStack()
    NBLK = GT // NB
    NSEQ = 4
    SB = 2 * TS

    def din(name, shape, dt=F32):
        return nc.dram_tensor(name, list(shape), dt, kind="ExternalInput").ap()

    def dout(name, shape, dt=F32):
        return nc.dram_tensor(name, list(shape), dt, kind="ExternalOutput").ap()

    xp = din("xp", [2, 128, KC, T])
    xs = din("xs", [128, KC, SB])
    cT_d = din("cT", [128, KC, NSEQ])
    w_ada = din("w_ada", [depth, D, 6 * D])
    w_in = din("w_in", [depth, D, DIN])
    w_pool = din("w_pool", [depth, 4, 128, 128])
    w_out = din("w_out", [depth, D, D])
    w_up = din("w_up", [depth, D, 2 * DFF])
    w_down = din("w_down", [depth, DFF, D])
    pv_d = din("pv", [128, NPV])
    bg_d = din("bg", [4, depth * 2])
    ident_d = din("ident", [128, 128])
    mask_d = din("mask", [128, 512])
    sel_d = din("sel", [128, 256])
    rc_d = din("rc", [128, 64])
    sp_hist = din("sp_hist", [128, depth, 2, 4, 15])
    sp_C = din("sp_C", [128, depth, 2, NH, 129])
    sp_mrow = din("sp_mrow", [depth, 2, 4, 1])
    sp_mbc = din("sp_mbc", [128, depth, 2, NH])
    sp_conv = din("sp_conv", [128, depth, 2, NFT, 2])
    yp = dout("yp", [2, 128, KC, T])
    ys = dout("ys", [128, KC, SB])
    o_pool = dout("o_pool", [128, depth, NSEQ, 4, 15])
    o_C = dout("o_C", [128, depth, NSEQ, NH, 129])
    o_m = dout("o_m", [depth, NSEQ, 4, 1])
    o_conv = dout("o_conv", [128, depth, NSEQ, NFT, 2])

    S = Sched(nc, es)
    S.tick_buf = Buf("pooltick", es.enter_context(nc.sbuf_tensor("pooltick", [128, 1], F32))[:])

    def sbt(name, shape, dt=F32):
        t = es.enter_context(nc.sbuf_tensor(name, list(shape), dt))
        return t

    def sbuf(name, shape, dt=F32):
        t = sbt(name, shape, dt)
        return Buf(name, t[:])

    SBLK = NBLK
    xT_t = sbt("xT", [128, KC, GT + SB])
    xT = [[Buf(f"xT{kc}_{b}", xT_t[:, kc, b * NB:(b + 1) * NB]) for b in range(NBLK)] +
          [Buf(f"xT{kc}_s", xT_t[:, kc, GT:GT + SB])] for kc in range(KC)]
    xT_all = [xT[kc][b] for kc in range(KC) for b in range(NBLK)]
    hT_t = [sbt(f"hT{i}", [128, KC, NB], BF16) for i in range(2)]
    hT = [[Buf(f"hT{i}_{kc}", hT_t[i][:, kc, :]) for kc in range(KC)] for i in range(2)]
    hTs_t = sbt("hTs", [128, KC, SB], BF16)
    hT.append([Buf(f"hTs_{kc}", hTs_t[:, kc, :]) for kc in range(KC)])
    sq = hT[1]
    rstd = sbuf("rstd", [128, NB])
    ntmp = [sbuf(f"ntmp{i}", [128, NB]) for i in range(2)]
    pv = sbuf("pvs", [128, NPV])
    ident = sbuf("ident_s", [128, 128])
    identb = sbuf("identb", [128, 128], BF16)
    onesb = sbuf("onesb", [128, 128], BF16)
    mask = sbuf("mask_s", [128, 512], BF16)
    mask_f = None
    sel = sbuf("sel_s", [128, 256])
    rc = sbuf("rc_s", [128, 64])
    bg = sbuf("bg_s", [4, depth * 2])
    nbf = sbuf("nbf", [4, depth])
    ones4c = sbuf("ones4c", [4, 1])
    cTs = sbuf("cTs", [128, KC, NSEQ])
    scT = sbuf("scT", [128, KC, NSEQ], BF16)
    mod = [sbuf(f"mod{l}", [128, 48, NSEQ]) for l in range(depth)]
    A1 = [sbuf(f"A1_{l}", [128, KC, NSEQ]) for l in range(depth)]
    A2 = [sbuf(f"A2_{l}", [128, KC, NSEQ]) for l in range(depth)]
    zero_col = sbuf("zero_col", [128, 1])
    junk = sbuf("junk", [128, 128], BF16)

    class St:
        pass
    states = {}
    for slot in range(3):
        for l in range(depth):
            st = St()
            n = f"{slot}_{l}"
            st.hist = sbuf("hist" + n, [128, 4, 15])
            st.Dm = sbuf("Dm" + n, [128, NH, 129])
            st.Cbf = sbuf("Cbf" + n, [128, NH, 129], BF16)
            st.glast = sbuf("glast" + n, [128, NH])
            st.Eprev = sbuf("Eprev" + n, [128, NH])
            st.Bprev = sbuf("Bprev" + n, [4, 1])
            st.mprev = sbuf("mprev" + n, [4, 1])
            st.cc = sbuf("cc" + n, [128, NFT, 2])
            states[(slot, l)] = st

    class Sm:
        pass
    smalls = []
    for i in range(3):
        sm = Sm()
        sm.dn = sbuf(f"dn_{i}", [128, 4])
        sm.a1 = sbuf(f"a1_{i}", [128, 4])
        sm.scale = sbuf(f"scale_{i}", [128, 4])
        sm.ssq = sbuf(f"ssq_{i}", [128, 4])
        sm.t2 = sbuf(f"t2_{i}", [128, 4])
        sm.comb = sbuf(f"comb_{i}", [128, 4])
        smalls.append(sm)

    NTMAX = max(NB // 128, 2)

    class Bs:
        pass
    bsets = []
    for i in range(2):
        bs = Bs()
        bs.tok = sbuf(f"btok{i}", [128, NTMAX, 12])
        bs.epv = sbuf(f"bepv{i}", [128, NTMAX, 4])
        bs.t1 = sbuf(f"bt1{i}", [128, NTMAX, 24])
        bs.ex = sbuf(f"bex{i}", [128, NTMAX, 24])
        bsets.append(bs)
    blk_ctr = [0]

    used = 208832 - nc.sbuf_bytes_remaining
    ARENA_BYTES = 114 * 1024
    arena = sbt("arena", [128, ARENA_BYTES // 4])

    class Layout:
        def __init__(self):
            self.off = 0

        def alloc(self, name, shape, dt=F32):
            esz = 4 if dt == F32 else 2
            n = int(np.prod(shape[1:])) * esz
            n4 = (n + 3) // 4
            assert self.off + n4 <= ARENA_BYTES // 4, (name, self.off * 4, n)
            ap = arena[:, self.off:self.off + n4]
            if dt != F32:
                ap = ap.bitcast(dt)
            ne = int(np.prod(shape[1:]))
            ap = ap[:, 0:ne]
            if len(shape) == 3:
                ap = ap.rearrange("p (a b) -> p a b", a=shape[1])
            elif len(shape) == 4:
                ap = ap.rearrange("p (a b c) -> p a b c", a=shape[1], b=shape[2])
            self.off += n4
            if shape[0] < 128:
                ap = ap[0:shape[0]]
            return Buf(name, ap)

    L0 = Layout()
    wada_s = [L0.alloc(f"wada{i}", [128, KC, 1024], BF16) for i in range(2)]

    L1 = Layout()
    wi_u = L1.alloc("wi_u", [128, KC, 512], BF16)
    wi_q = L1.alloc("wi_q", [128, KC, 512], BF16)
    wi_k = L1.alloc("wi_k", [128, KC, 512], BF16)
    wi_vo = L1.alloc("wi_vo", [128, KC, 1024], BF16)
    wi_g = L1.alloc("wi_g", [128, KC, 8], BF16)
    wo_s = L1.alloc("wo_s", [128, KC, 1024], BF16)
    wp_s = L1.alloc("wp_s", [128, 4, 128], BF16)
    qT = [L1.alloc(f"qT{h}", [128, NB], BF16) for h in range(NH)]
    kT = [L1.alloc(f"kT{h}", [128, NB], BF16) for h in range(NH)]
    UT = [[L1.alloc(f"UT{s}_{g}", [128, 15 + (NB if s == 0 else TS)]) for g in range(4)] for s in range(2)]
    ptmp = [L1.alloc(f"ptmp{i}", [128, 15 + NB]) for i in range(2)]
    pfix = L1.alloc("pfix", [128, 16])
    diffT = [L1.alloc(f"diffT{g}", [128, NB], BF16) for g in range(4)]
    mixT = [L1.alloc(f"mixT{k}", [128, NB], BF16) for k in range(KC)]
    igs = L1.alloc("igs", [4, NB])
    lfs = L1.alloc("lfs", [4, NB])
    e1s = lfs
    Brow = L1.alloc("Brow", [4, NB])
    Mrow = L1.alloc("Mrow", [4, NB])
    Erow = Brow
    Grow = igs
    vaug = [L1.alloc(f"vaug{i}", [128, NH, 129], BF16) for i in range(3)]
    ktok = [L1.alloc(f"ktok{i}", [128, 512], BF16) for i in range(3)]
    go = [L1.alloc(f"go{i}", [128, 512]) for i in range(3)]
    SmT = [L1.alloc(f"SmT{i}", [128, NH, 128], BF16) for i in range(2)]
    hs = [L1.alloc("hs0", [128, 512], BF16)] * 2

    L2 = Layout()
    actT = [[L2.alloc(f"actT{j}_{b}", [128, NB], BF16) for b in range(NBLK)] + [L2.alloc(f"actT{j}_s", [128, SB], BF16)]
            for j in range(NKT)]
    wup_s = [L2.alloc(f"wup{i}", [128, KC, 512], BF16) for i in range(2)]
    wdn_s = [L2.alloc(f"wdn{i}", [128, NKT, 128], BF16) for i in range(3)]
    acc = [L2.alloc(f"acc{i}", [128, NB]) for i in range(6)]
    sgt = [L2.alloc(f"sg{i}", [128, NB]) for i in range(2)]
    UPt = [L2.alloc(f"UP{i}", [128, NB + 4]) for i in range(4)]
    UPp = [Buf(f"UPp{i}", UPt[i].ap) for i in range(4)]
    NACC = 6

    L3 = Layout()
    yT = [L3.alloc(f"yT{i}", [128, KC, NB]) for i in range(2)]
    stC = L3.alloc("stC", [128, NH, 129])

    PS = []
    for i in range(8):
        t = es.enter_context(nc.psum_tensor(f"ps{i}", [128, 512], F32))
        PS.append(Buf(f"ps{i}", t[:]))
        PS[-1].excl = True

    def pvcol(l, off, n=1):
        base = l * PVL + off
        return pv[:, base:base + n]

    OFF_BADA, OFF_G1, OFF_G2, OFF_SPOOL, OFF_GHEAD, OFF_WCONV, OFF_BCONV = 0, 48, 56, 64, 68, 72, 204

    fm_rr = [0]

    def fm_bank(banks):
        b = banks[fm_rr[0] % len(banks)]
        fm_rr[0] += 1
        return b

    S.dma("sp", pv, pv_d)
    S.dma("sp", ident, ident_d)
    S.dma("pool", mask, mask_d)
    S.dma("sp", sel, sel_d)
    S.dma("sp", rc, rc_d)
    S.dma("sp", bg, bg_d)
    S.dma("sp", cTs, cT_d)
    S.op("dve", "tensor_copy", out=identb, in_=ident)
    S.op("dve", "memset", ap=onesb, constant=1.0)
    S.op("dve", "memset", ap=ones4c, constant=1.0)
    S.op("dve", "memset", ap=zero_col, constant=0.0)
    for l in range(depth):
        S.op("dve", "tensor_scalar", out=nbf[:, l:l + 1], in0=bg[:, 2 * l + 1:2 * l + 2], scalar1=-1.0,
             scalar2=None, op0=ALU.mult)
    S.op("act", "activation", out=scT, in_=cTs, func=AF.Silu)

    for l in range(depth):
        wv = w_ada[l].rearrange("(kc p) n -> p kc n", p=128)
        for piece in range(6):
            wb = wada_s[piece % 2]
            S.dma("pool", wb, wv[:, :, piece * 1024:(piece + 1) * 1024])
            for fcl in range(8):
                fc = piece * 8 + fcl
                for kc in range(KC):
                    S.op("pe", "matmul", signal=(kc == KC - 1), out=PS[0][:, fc * 4:fc * 4 + 4],
                         lhsT=wb[:, kc, fcl * 128:(fcl + 1) * 128], rhs=scT[:, kc, :],
                         start=(kc == 0), stop=(kc == KC - 1))
        S.op("dve", "tensor_tensor", out=mod[l], in0=PS[0][:, 0:192].rearrange("p (a b) -> p a b", b=NSEQ),
             in1=pvcol(l, OFF_BADA, 48).rearrange("p (a b) -> p a b", b=1).bcast([128, 48, NSEQ]), op=ALU.add)
        for (A, goff, scj) in ((A1[l], OFF_G1, 1), (A2[l], OFF_G2, 4)):
            S.op("dve", "tensor_scalar", out=A, in0=mod[l][:, scj * 8:(scj + 1) * 8, :], scalar1=1.0, scalar2=None,
                 op0=ALU.add)
            S.op("dve", "tensor_tensor", out=A, in0=A,
                 in1=pvcol(l, goff, 8).rearrange("p (a b) -> p a b", b=1).bcast([128, 8, NSEQ]), op=ALU.mult)
    S.barrier()

    def norm(blk, ncols, segs, scale_fn, bias_fn, outs):
        for kc in range(KC):
            S.op("act", "activation", out=sq[kc][:, 0:ncols], in_=xT[kc][blk][:, 0:ncols], func=AF.Square)
        for kc in range(KC):
            S.op("pe", "matmul", signal=(kc == KC - 1), out=PS[0][:, 0:ncols], lhsT=onesb, rhs=sq[kc][:, 0:ncols],
                 start=(kc == 0), stop=(kc == KC - 1))
        S.op("act", "activation", out=rstd[:, 0:ncols], in_=PS[0][:, 0:ncols], func=AF.Ln, scale=1.0 / D, bias=EPS)
        S.op("act", "activation", out=rstd[:, 0:ncols], in_=rstd[:, 0:ncols], func=AF.Exp, scale=-0.5)
        for kc in range(KC):
            nt = ntmp[kc % 2]
            S.op("dve", "tensor_tensor", out=nt[:, 0:ncols], in0=xT[kc][blk][:, 0:ncols], in1=rstd[:, 0:ncols],
                 op=ALU.mult)
            for (c0, c1, q) in segs:
                S.op("act", "activation", out=outs[kc][:, c0:c1], in_=nt[:, c0:c1], func=AF.Identity,
                     scale=scale_fn(kc, q), bias=bias_fn(kc, q))

    def load_w1(l):
        wv = w_in[l].rearrange("(kc p) n -> p kc n", p=128)
        S.dma("pool", wi_g, wv[:, :, 2560:2568])
        S.dma("pool", wi_k, wv[:, :, 1024:1536])
        S.dma("pool", wi_q, wv[:, :, 512:1024])
        S.dma("pool", wi_u, wv[:, :, 0:512])
        S.dma("pool", wi_vo, wv[:, :, 1536:2560])
        S.dma("pool", wp_s, w_pool[l].rearrange("g c d -> c g d"))
        S.dma("pool", wo_s, w_out[l].rearrange("(kc p) n -> p kc n", p=128))

    def phase1_block(l, blk, ncols, segs, tiles, first_block):
        h = hT[0]
        norm(blk, ncols, [(c0, c1, q) for (c0, c1, q, _) in segs],
             lambda kc, q: A1[l][:, kc, q:q + 1], lambda kc, q: mod[l][:, 0 * 8 + kc, q:q + 1], h)
        for (ps, c0) in ((PS[3], 0), (PS[4], 4)):
            for kc in range(KC):
                S.op("pe", "matmul", signal=(kc == KC - 1), out=ps[0:4, 0:ncols], lhsT=wi_g[:, kc, c0:c0 + 4],
                     rhs=h[kc][:, 0:ncols], start=(kc == 0), stop=(kc == KC - 1))
        S.op("act", "activation", out=igs[:, 0:ncols], in_=PS[3][0:4, 0:ncols], func=AF.Identity,
             bias=bg[:, 2 * l:2 * l + 1], scale=1.0)
        S.op("act", "activation", out=e1s[:, 0:ncols], in_=PS[4][0:4, 0:ncols], func=AF.Exp,
             bias=nbf[:, l:l + 1], scale=-1.0)
        S.op("act", "activation", out=e1s[:, 0:ncols], in_=e1s[:, 0:ncols], func=AF.Ln, bias=1.0, scale=1.0)
        S.op("dve", "tensor_scalar", out=lfs[:, 0:ncols], in0=e1s[:, 0:ncols], scalar1=-1.0, scalar2=None,
             op0=ALU.mult)
        for (c0, c1, q, slot) in segs:
            st = states[(slot, l)]
            S.op("dve", "tensor_tensor_scan", out=Brow[:, c0:c1], data0=ones4c.bcast([4, c1 - c0]), data1=lfs[:, c0:c1],
                 initial=st.Bprev[:, 0:1], op0=ALU.mult, op1=ALU.add)
            S.op("dve", "tensor_tensor_scan", out=Mrow[:, c0:c1], data0=lfs[:, c0:c1], data1=igs[:, c0:c1],
                 initial=st.mprev[:, 0:1], op0=ALU.add, op1=ALU.max)
            S.op("dve", "tensor_copy", out=st.Bprev, in_=Brow[:, c1 - 1:c1])
            S.op("dve", "tensor_copy", out=st.mprev, in_=Mrow[:, c1 - 1:c1])
        S.op("dve", "tensor_tensor", out=Grow[:, 0:ncols], in0=igs[:, 0:ncols], in1=Brow[:, 0:ncols], op=ALU.subtract)
        S.op("dve", "tensor_tensor", out=Erow[:, 0:ncols], in0=Brow[:, 0:ncols], in1=Mrow[:, 0:ncols], op=ALU.subtract)

        def proj_fm(wbuf, m):
            ps = fm_bank([PS[1], PS[2]])
            for kc in range(KC):
                S.op("pe", "matmul", signal=(kc == KC - 1), out=ps[:, 0:ncols], lhsT=wbuf[:, kc, m * 128:(m + 1) * 128],
                     rhs=h[kc][:, 0:ncols], start=(kc == 0), stop=(kc == KC - 1))
            return ps
        for hh in range(NH):
            ps = proj_fm(wi_k, hh)
            S.op("act", "activation", out=kT[hh][:, 0:ncols], in_=ps[:, 0:ncols], func=AF.Copy, scale=float(HD ** -0.5))
        for hh in range(NH):
            ps = proj_fm(wi_q, hh)
            S.op("act", "activation", out=qT[hh][:, 0:ncols], in_=ps[:, 0:ncols], func=AF.Copy)
        for g in range(4):
            ps = proj_fm(wi_u, g)
            for si, (c0, c1, q, slot) in enumerate(segs):
                S.op("act", "activation", out=UT[si][g][:, 15:15 + (c1 - c0)], in_=ps[:, c0:c1], func=AF.Copy)

        for si, (c0, c1, q, slot) in enumerate(segs):
            st = states[(slot, l)]
            n = c1 - c0
            W = 15 + n
            for g in range(4):
                S.op("pool", "tensor_copy", out=UT[si][g][:, 0:15], in_=st.hist[:, g, :])
            for g, w in enumerate(WINDOWS):
                U = UT[si][g]
                nlev = {2: 1, 4: 2, 8: 3, 16: 4}[w]
                starts = [0] * (nlev + 1)
                starts[nlev] = 15
                for j in range(nlev, 1, -1):
                    starts[j - 1] = starts[j] - 2 ** (j - 1)
                src = U
                for j in range(1, nlev + 1):
                    dst = ptmp[j % 2]
                    s0 = starts[j]
                    sh = 2 ** (j - 1)
                    S.op("pool", "tensor_tensor", out=dst[:, s0:W], in0=src[:, s0:W], in1=src[:, s0 - sh:W - sh], op=ALU.add)
                    src = dst
                other = ptmp[(nlev + 1) % 2]
                S.op("pool", "tensor_scalar", out=other[:, 15:W], in0=src[:, 15:W], scalar1=1.0 / w, scalar2=0.0,
                     op0=ALU.mult, op1=ALU.add)
                S.op("pool", "tensor_tensor", out=diffT[g][:, c0:c1], in0=other[:, 15:W], in1=U[:, 15:W], op=ALU.subtract)
                if first_block:
                    S.op("pool", "tensor_tensor", out=pfix, in0=src[:, 15:31], in1=rc[:, 16 * g:16 * g + 16], op=ALU.mult)
                    S.op("pool", "tensor_tensor", out=diffT[g][:, c0:c0 + 16], in0=pfix, in1=U[:, 15:31], op=ALU.subtract)
                S.op("pool", "tensor_copy", out=st.hist[:, g, :], in_=U[:, W - 15:W])

        Pbanks = [(PS[3], PS[4]), (PS[5], PS[6])]
        RB = [PS[0], PS[1], PS[2], PS[7]]
        bs = bsets[blk_ctr[0] % 2]
        blk_ctr[0] += 1
        nt = len(tiles)
        LL = tiles[0][1]
        selL = sel[0:LL, 0:128] if LL == 128 else sel[0:LL, 128:256]
        EB0 = 12 * NTMAX
        for ti, (c0, L, q, slot) in enumerate(tiles):
            for j, row in enumerate((Erow, Grow, Mrow)):
                S.op("pe", "transpose", signal=(ti == nt - 1 and j == 2), out=PS[0][0:L, 12 * ti + 4 * j:12 * ti + 4 * j + 4],
                     in_=row[:, c0:c0 + L], identity=ident[0:4, 0:4])
        S.op("dve", "tensor_copy", out=bs.tok[0:LL, 0:nt, :],
             in_=PS[0][0:LL, 0:12 * nt].rearrange("p (a b) -> p a b", b=12))
        for ti in range(nt):
            S.op("pe", "matmul", signal=(ti == nt - 1), out=PS[0][:, EB0 + 4 * ti:EB0 + 4 * ti + 4], lhsT=selL,
                 rhs=bs.tok[0:LL, ti, 0:4], start=True, stop=True)
        ebc = PS[0][:, EB0:EB0 + 4 * nt].rearrange("p (a b) -> p a b", b=4)
        prev_of = {}
        last_of = {}
        for ti, (c0, L, q, slot) in enumerate(tiles):
            prev_of[ti] = last_of.get(slot)
            last_of[slot] = ti
        ti = 0
        while ti < nt:
            if prev_of[ti] is None:
                S.op("dve", "tensor_copy", out=bs.epv[:, ti, :], in_=states[(tiles[ti][3], l)].Eprev)
                ti += 1
            else:
                t_end = ti
                while t_end < nt and prev_of[t_end] == t_end - 1:
                    t_end += 1
                S.op("dve", "tensor_copy", out=bs.epv[:, ti:t_end, :], in_=ebc[:, ti - 1:t_end - 1, :])
                ti = t_end
        S.op("dve", "tensor_tensor", out=bs.t1[0:LL, 0:nt, 0:4], in0=bs.tok[0:LL, 0:nt, 4:8], in1=bs.epv[0:LL, 0:nt, :],
             op=ALU.add)
        S.op("dve", "tensor_tensor", out=bs.t1[0:LL, 0:nt, 4:8], in0=bs.tok[0:LL, 0:nt, 0:4], in1=bs.epv[0:LL, 0:nt, :],
             op=ALU.subtract)
        S.op("dve", "scalar_tensor_tensor", out=bs.t1[0:LL, 0:nt, 8:12], in0=bs.t1[0:LL, 0:nt, 4:8], scalar=-1.0,
             in1=bs.tok[0:LL, 0:nt, 8:12], op0=ALU.mult, op1=ALU.subtract)
        S.op("dve", "tensor_scalar", out=bs.t1[0:LL, 0:nt, 12:16], in0=bs.t1[0:LL, 0:nt, 8:12], scalar1=2.0, scalar2=None,
             op0=ALU.mult)
        S.op("dve", "tensor_scalar", out=bs.t1[0:LL, 0:nt, 16:20], in0=bs.t1[0:LL, 0:nt, 4:8], scalar1=2.0, scalar2=None,
             op0=ALU.mult)
        S.op("dve", "tensor_tensor", out=bs.t1[:, 0:nt, 20:24], in0=ebc, in1=bs.epv[:, 0:nt, :], op=ALU.subtract)
        S.op("act", "activation", out=bs.ex[0:LL, 0:nt, 0:20], in_=bs.t1[0:LL, 0:nt, 0:20], func=AF.Exp)
        S.op("act", "activation", out=bs.ex[:, 0:nt, 20:24], in_=bs.t1[:, 0:nt, 20:24], func=AF.Exp)
        for slot, ti in last_of.items():
            S.op("dve", "tensor_copy", out=states[(slot, l)].Eprev, in_=ebc[:, ti, :])

        def stageA(ti):
            (c0, L, q, slot) = tiles[ti]
            st = states[(slot, l)]
            sm = smalls[ti % 3]
            p2, p3 = ti % 2, ti % 3
            cs = slice(c0, c0 + L)
            u_ = bs.ex[0:L, ti, 0:4]
            va = vaug[p3]
            psv = fm_bank(RB)
            for kc in range(KC):
                S.op("pe", "matmul", signal=(kc == KC - 1), out=psv[0:L, :], lhsT=h[kc][:, cs], rhs=wi_vo[:, kc, 0:512],
                     start=(kc == 0), stop=(kc == KC - 1))
            S.op("dve", "tensor_tensor", out=va[0:L, :, 0:128], in0=psv[0:L, :].rearrange("p (a b) -> p a b", a=NH),
                 in1=u_.rearrange("p (a b) -> p a b", b=1).bcast([L, NH, 128]), op=ALU.mult)
            S.op("dve", "tensor_copy", out=va[0:L, :, 128:129], in_=u_.rearrange("p (a b) -> p a b", b=1))
            pso = fm_bank(RB)
            for kc in range(KC):
                S.op("pe", "matmul", signal=(kc == KC - 1), out=pso[0:L, :], lhsT=h[kc][:, cs], rhs=wi_vo[:, kc, 512:1024],
                     start=(kc == 0), stop=(kc == KC - 1))
            S.op("act", "activation", out=go[p3][0:L, :], in_=pso[0:L, :], func=AF.Exp, scale=-1.0)
            S.op("act", "activation", out=go[p3][0:L, :], in_=go[p3][0:L, :], func=AF.Ln, bias=1.0, scale=1.0)
            S.op("act", "activation", out=go[p3][0:L, :], in_=go[p3][0:L, :], func=AF.Exp, scale=-1.0)
            pst = fm_bank(RB)
            pstb = pst.bitcast(BF16)
            for hh in range(NH):
                S.op("pe", "transpose", signal=(hh == NH - 1), out=pstb[0:L, hh * 128:(hh + 1) * 128], in_=kT[hh][:, cs],
                     identity=identb)
            S.op("act", "activation", out=ktok[p3][0:L, :], in_=pstb[0:L, 0:512], func=AF.Copy)
            pss = fm_bank(RB)
            for hh in range(NH):
                S.op("pe", "matmul", signal=(hh == NH - 1), out=pss[0:L, hh * 128:hh * 128 + L], lhsT=kT[hh][:, cs],
                     rhs=qT[hh][:, cs], start=True, stop=True)
            S.op("dve", "tensor_tensor", out=SmT[p2][0:L, :, 0:L],
                 in0=pss[0:L, :].rearrange("p (a b) -> p a b", a=NH)[:, :, 0:L],
                 in1=mask[0:L, :].rearrange("p (a b) -> p a b", a=NH)[:, :, 0:L], op=ALU.mult)

        def stageP(ti):
            (c0, L, q, slot) = tiles[ti]
            st = states[(slot, l)]
            sm = smalls[ti % 3]
            p2, p3 = ti % 2, ti % 3
            cs = slice(c0, c0 + L)
            va = vaug[p3]
            PB = Pbanks[ti % 2]
            for hh in range(NH):
                ps = PB[hh // 2]
                o0 = (hh % 2) * 129
                S.op("pe", "matmul", signal=False, out=ps[0:L, o0:o0 + 129], lhsT=SmT[p2][0:L, hh, 0:L],
                     rhs=va[0:L, hh, :], start=True, stop=False)
                S.op("pe", "matmul", signal=True, out=ps[0:L, o0:o0 + 129], lhsT=qT[hh][:, cs], rhs=st.Cbf[:, hh, :],
                     start=False, stop=True)

        def stageQ1(ti):
            (c0, L, q, slot) = tiles[ti]
            sm = smalls[ti % 3]
            PB = Pbanks[ti % 2]
            for hh in range(NH):
                ps = PB[hh // 2]
                o0 = (hh % 2) * 129
                S.op("act", "activation", out=junk[0:L, :], in_=ps[0:L, o0:o0 + 128], func=AF.Square, scale=float(HD ** -0.5),
                     accum_out=sm.ssq[0:L, hh:hh + 1])

        def stageQ(ti):
            (c0, L, q, slot) = tiles[ti]
            st = states[(slot, l)]
            sm = smalls[ti % 3]
            p2, p3 = ti % 2, ti % 3
            cs = slice(c0, c0 + L)
            emg2 = bs.ex[0:L, ti, 12:16]
            PB = Pbanks[ti % 2]
            for pi in range(2):
                S.op("dve", "tensor_copy", out=sm.dn[0:L, 2 * pi:2 * pi + 2],
                     in_=PB[pi][0:L, 0:258].rearrange("p (a b) -> p a b", b=129)[:, :, 128])
            S.op("dve", "tensor_tensor", out=sm.a1[0:L, :], in0=sm.dn[0:L, :], in1=sm.dn[0:L, :], op=ALU.mult)
            S.op("dve", "tensor_tensor", out=sm.t2[0:L, :], in0=sm.a1[0:L, :], in1=emg2, op=ALU.max)
            S.op("dve", "scalar_tensor_tensor", out=sm.t2[0:L, :], in0=sm.t2[0:L, :], scalar=EPS, in1=sm.ssq[0:L, :],
                 op0=ALU.mult, op1=ALU.add)
            S.op("dve", "tensor_tensor", out=sm.t2[0:L, :], in0=sm.t2[0:L, :], in1=bs.ex[0:L, ti, 16:20], op=ALU.mult)
            S.op("act", "activation", out=sm.t2[0:L, :], in_=sm.t2[0:L, :], func=AF.Ln)
            S.op("act", "activation", out=sm.scale[0:L, :], in_=sm.t2[0:L, :], func=AF.Exp, scale=-0.5)
            S.op("dve", "tensor_tensor", out=sm.comb[0:L, :], in0=sm.scale[0:L, :], in1=bs.ex[0:L, ti, 4:8], op=ALU.mult)
            for hh in range(NH):
                ps = PB[hh // 2]
                o0 = (hh % 2) * 129
                S.op("dve", "scalar_tensor_tensor", out=hs[p2][0:L, hh * 128:(hh + 1) * 128], in0=ps[0:L, o0:o0 + 128],
                     scalar=sm.comb[0:L, hh:hh + 1], in1=go[p3][0:L, hh * 128:(hh + 1) * 128], op0=ALU.mult, op1=ALU.mult)
            p7b = fm_bank(RB).bitcast(BF16)
            for hh in range(NH):
                S.op("pe", "transpose", signal=(hh == NH - 1), out=p7b[:, hh * 128:hh * 128 + L],
                     in_=hs[p2][0:L, hh * 128:(hh + 1) * 128], identity=identb[0:L, 0:L])
            for hh in range(NH):
                S.op("act", "activation", out=mixT[4 + hh][:, cs], in_=p7b[:, hh * 128:hh * 128 + L], func=AF.Identity,
                     scale=pvcol(l, OFF_GHEAD + hh))

        def stageS(ti):
            (c0, L, q, slot) = tiles[ti]
            st = states[(slot, l)]
            p2, p3 = ti % 2, ti % 3
            va = vaug[p3]
            SB2 = [fm_bank(RB), fm_bank(RB)]
            for hh in range(NH):
                ps = SB2[hh // 2]
                o0 = (hh % 2) * 129
                S.op("pe", "matmul", out=ps[:, o0:o0 + 129], lhsT=ktok[p3][0:L, hh * 128:(hh + 1) * 128],
                     rhs=va[0:L, hh, :], start=True, stop=True)
            gl = bs.ex[:, ti, 20:24].rearrange("p (a b) -> p a b", b=1).bcast([128, NH, 129])
            for pi in range(2):
                S.op("dve", "tensor_tensor", out=st.Dm[:, 2 * pi:2 * pi + 2, :], in0=st.Dm[:, 2 * pi:2 * pi + 2, :],
                     in1=SB2[pi][:, 0:258].rearrange("p (a b) -> p a b", b=129), op=ALU.add)
            S.op("dve", "tensor_tensor", out=st.Cbf, in0=st.Dm, in1=gl, op=ALU.mult)
            S.op("dve", "tensor_tensor", out=st.Dm, in0=st.Dm, in1=gl, op=ALU.mult)

        stageA(0)
        if nt > 1:
            stageA(1)
        for ti in range(nt):
            stageP(ti)
            stageS(ti)
            stageQ1(ti)
            if ti >= 1:
                stageQ(ti - 1)
            if ti + 2 < nt:
                stageA(ti + 2)
        stageQ(nt - 1)

        for g in range(4):
            ps = fm_bank([PS[1], PS[2]])
            S.op("pe", "matmul", out=ps[:, 0:ncols], lhsT=wp_s[:, g, :], rhs=diffT[g][:, 0:ncols], start=True, stop=True)
            S.op("act", "activation", out=mixT[g][:, 0:ncols], in_=ps[:, 0:ncols], func=AF.Identity,
                 scale=pvcol(l, OFF_SPOOL + g))
        for m in range(KC):
            ps = fm_bank([PS[1], PS[2]])
            for kc in range(KC):
                S.op("pe", "matmul", signal=(kc == KC - 1), out=ps[:, 0:ncols], lhsT=wo_s[:, kc, m * 128:(m + 1) * 128],
                     rhs=mixT[kc][:, 0:ncols], start=(kc == 0), stop=(kc == KC - 1))
            for (c0, c1, q, slot) in segs:
                S.op("dve", "scalar_tensor_tensor", out=xT[m][blk][:, c0:c1], in0=ps[:, c0:c1],
                     scalar=mod[l][:, 2 * 8 + m, q:q + 1], in1=xT[m][blk][:, c0:c1], op0=ALU.mult, op1=ALU.add)

    def phase2(l, blocks):
        for (blk, ncols, segs) in sorted(blocks, key=lambda b: (b[0] == 1, b[0])):
            norm(blk, ncols, [(c0, c1, q) for (c0, c1, q, _) in segs],
                 lambda kc, q: A2[l][:, kc, q:q + 1], lambda kc, q: mod[l][:, 3 * 8 + kc, q:q + 1], hT[blk])
        banks = [PS[i] for i in range(1, 8)]
        wv = w_up[l].rearrange("(kc p) (h j c) -> p kc h j c", p=128, h=2, c=128)
        acc_rr = 0
        def load_up(jj):
            wbx = wup_s[jj % 2].rearrange("p k (h j c) -> p k h j c", h=2, j=2)
            for half in range(2):
                S.dma("pool", wbx[:, :, half], wv[:, :, half, 2 * jj:2 * jj + 2, :])
        load_up(0)
        wdv = w_down[l].rearrange("(kt p) n -> p kt n", p=128)
        S.dma("pool", wdn_s[0], wdv[:, :, 0:128])
        S.dma("pool", wdn_s[1], wdv[:, :, 128:256])
        S.dma("pool", wdn_s[2], wdv[:, :, 256:384])
        pending = []

        def back(item):
            (j, blk, ncols, accs) = item
            sg = sgt[(j * len(blocks) + blk) % 2]
            S.op("act", "activation", out=sg[:, 0:ncols], in_=accs[1][:, 0:ncols], func=AF.Silu)
            S.op("dve", "tensor_tensor", out=actT[j][blk][:, 0:ncols], in0=accs[0][:, 0:ncols], in1=sg[:, 0:ncols],
                 op=ALU.mult)

        for jj in range(NKT // 2):
            wb = wup_s[jj % 2]
            wb4 = wb.rearrange("p k (h j c) -> p k h j c", h=2, j=2)
            if jj + 1 < NKT // 2:
                load_up(jj + 1)
            for jl in range(2):
                j = 2 * jj + jl
                for (blk, ncols, segs) in blocks:
                    accs = []
                    for half in range(2):
                        f = half * NKT + j
                        ps = fm_bank(banks)
                        for kc in range(KC):
                            S.op("pe", "matmul", signal=(kc == KC - 1), out=ps[:, 0:ncols], lhsT=wb4[:, kc, half, jl, :],
                                 rhs=hT[blk][kc][:, 0:ncols], start=(kc == 0), stop=(kc == KC - 1))
                        a = acc[acc_rr % NACC]
                        up = UPt[acc_rr % 4]
                        upp = UPp[acc_rr % 4]
                        acc_rr += 1
                        accs.append(a)
                        w0 = pvcol(l, OFF_WCONV + 0 * NFT + f)
                        w1 = pvcol(l, OFF_WCONV + 1 * NFT + f)
                        w2 = pvcol(l, OFF_WCONV + 2 * NFT + f)
                        bc = pvcol(l, OFF_BCONV + f)
                        S.op("act", "activation", out=a[:, 0:ncols], in_=ps[:, 0:ncols], func=AF.Identity, scale=w2, bias=bc)
                        for si, (c0, c1, q, slot) in enumerate(segs):
                            st = states[(slot, l)]
                            off = c0 + 2 * si
                            n = c1 - c0
                            S.op("dve", "tensor_copy", out=View(upp, up.ap[:, off:off + 2]), in_=st.cc[:, f, :])
                            S.op("act", "activation", out=up[:, off + 2:off + 2 + n], in_=ps[:, c0:c1], func=AF.Copy)
                        for si, (c0, c1, q, slot) in enumerate(segs):
                            st = states[(slot, l)]
                            off = c0 + 2 * si
                            n = c1 - c0
                            S.op("dve", "tensor_copy", out=st.cc[:, f, :], in_=up[:, off + n:off + n + 2])
                            S.op("dve", "scalar_tensor_tensor", out=a[:, c0:c1], in0=up[:, off + 1:off + 1 + n].also(upp),
                                 scalar=w1, in1=a[:, c0:c1], op0=ALU.mult, op1=ALU.add)
                            S.op("dve", "scalar_tensor_tensor", out=a[:, c0:c1], in0=up[:, off:off + n].also(upp), scalar=w0,
                                 in1=a[:, c0:c1], op0=ALU.mult, op1=ALU.add)
                    for it in pending:
                        back(it)
                    pending = [(j, blk, ncols, accs)]
        for it in pending:
            back(it)
        NLATE = 1
        opened = []
        for m in range(2):
            wb = wdn_s[m % 3]
            for (blk, ncols, segs) in blocks:
                ps = fm_bank(banks)
                for kt in range(NKT - NLATE):
                    S.op("pe", "matmul", signal=False, out=ps[:, 0:ncols], lhsT=wb[:, kt, :],
                         rhs=actT[kt][blk][:, 0:ncols], start=(kt == 0), stop=False)
                opened.append((m, wb, ps, blk, ncols, segs))
        for (m, wb, ps, blk, ncols, segs) in opened:
            for kt in range(NKT - NLATE, NKT):
                S.op("pe", "matmul", signal=(kt == NKT - 1), out=ps[:, 0:ncols], lhsT=wb[:, kt, :],
                     rhs=actT[kt][blk][:, 0:ncols], start=False, stop=(kt == NKT - 1))
            for (c0, c1, q, slot) in segs:
                S.op("dve", "scalar_tensor_tensor", out=xT[m][blk][:, c0:c1], in0=ps[:, c0:c1],
                     scalar=mod[l][:, 5 * 8 + m, q:q + 1], in1=xT[m][blk][:, c0:c1], op0=ALU.mult, op1=ALU.add)
        for mm in (3, 4):
            S.dma("pool", wdn_s[mm % 3], wdv[:, :, mm * 128:(mm + 1) * 128])
        for m in range(2, KC):
            wb = wdn_s[m % 3]
            if m >= 3 and m + 2 < KC:
                S.dma("pool", wdn_s[(m + 2) % 3], wdv[:, :, (m + 2) * 128:(m + 3) * 128])
            for (blk, ncols, segs) in blocks:
                ps = fm_bank(banks)
                for kt in range(NKT):
                    S.op("pe", "matmul", signal=(kt == NKT - 1), out=ps[:, 0:ncols], lhsT=wb[:, kt, :],
                         rhs=actT[kt][blk][:, 0:ncols], start=(kt == 0), stop=(kt == NKT - 1))
                for (c0, c1, q, slot) in segs:
                    S.op("dve", "scalar_tensor_tensor", out=xT[m][blk][:, c0:c1], in0=ps[:, c0:c1],
                         scalar=mod[l][:, 5 * 8 + m, q:q + 1], in1=xT[m][blk][:, c0:c1], op0=ALU.mult, op1=ALU.add)

    def init_prompt_states():
        for l in range(depth):
            st = states[(0, l)]
            for b in (st.hist, st.Dm, st.Cbf, st.Eprev, st.cc):
                S.op("dve", "memset", ap=b, constant=0.0)
            S.op("dve", "memset", ap=st.glast, constant=1.0)
            S.op("dve", "memset", ap=st.Bprev, constant=0.0)
            S.op("dve", "memset", ap=st.mprev, constant=0.0)

    def init_sample_states():
        for si in range(2):
            slot = 1 + si
            for l in range(depth):
                st = states[(slot, l)]
                S.dma("sp", st.hist, sp_hist[:, l, si])
                S.dma("sp", st.Dm, sp_C[:, l, si])
                S.dma("sp", st.mprev, sp_mrow[l, si])
                S.dma("sp", st.Eprev, sp_mbc[:, l, si])
                S.dma("sp", st.cc, sp_conv[:, l, si])
                S.op("dve", "tensor_copy", out=st.Cbf, in_=st.Dm)
                S.op("dve", "tensor_scalar", out=st.Eprev, in0=st.Eprev, scalar1=-1.0, scalar2=None, op0=ALU.mult)
                S.op("dve", "memset", ap=st.glast, constant=1.0)
                S.op("dve", "memset", ap=st.Bprev, constant=0.0)

    def store_states(slot, q):
        for l in range(depth):
            st = states[(slot, l)]
            S.dma("sp", o_pool[:, l, q], st.hist)
            S.dma("sp", o_C[:, l, q], st.Dm)
            S.dma("sp", o_m[l, q], st.mprev)
            S.dma("sp", o_conv[:, l, q], st.cc)

    def final_norm_store(blk, ncols, dst_ap, par):
        y = yT[par]
        outs = [Buf_view(y, kc) for kc in range(KC)]
        norm(blk, ncols, [(0, ncols, 0)], lambda kc, q: pv[:, 2 * PVL + kc:2 * PVL + kc + 1],
             lambda kc, q: 0.0, outs)
        S.dma("sp", dst_ap, y[:, :, 0:ncols])

    def Buf_view(y, kc):
        return y[:, kc, :]

    groups = []
    for s in range(2):
        for g in range(T // GT):
            groups.append((s, g))
    sample_segs = [(0, TS, 2, 1), (TS, 2 * TS, 3, 2)]
    sample_tiles = [(0, TS, 2, 1), (TS, TS, 3, 2)]

    for gi, (s, g) in enumerate(groups):
        last = gi == len(groups) - 1
        t0 = g * GT
        S.dma("sp", _multi([xT[kc][b] for kc in range(KC) for b in range(NBLK)], xT_t[:, :, 0:GT]), xp[s][:, :, t0:t0 + GT])
        if g == 0:
            init_prompt_states()
        blocks = [(b, NB, [(0, NB, s, 0)]) for b in range(NBLK)]
        if last:
            S.dma("sp", _multi([xT[kc][SBLK] for kc in range(KC)], xT_t[:, :, GT:GT + SB]), xs)
            init_sample_states()
            blocks.append((SBLK, SB, sample_segs))

        def tiles_of(b):
            return sample_tiles if b == SBLK else [(c, 128, s, 0) for c in range(0, NB, 128)]
        for l in range(depth):
            load_w1(l)
            for (blk, ncols, segs) in blocks:
                phase1_block(l, blk, ncols, segs, tiles_of(blk), first_block=(g == 0 and blk == 0))
            S.barrier()
            phase2(l, blocks)
            S.barrier()
        for bi, (blk, ncols, segs) in enumerate(blocks):
            if blk == SBLK:
                dst = ys
            else:
                dst = yp[s][:, :, t0 + blk * NB:t0 + (blk + 1) * NB]
            final_norm_store(blk, ncols, dst, bi % 2)
        if g == T // GT - 1:
            store_states(0, s)
        if last:
            store_states(1, 2)
            store_states(2, 3)
        S.barrier()

    block = es.enter_context(nc.Block())
    S.replay(block)
    es.close()
    return nc


class _MultiView(View):
    pass


def _multi(bufs, ap):
    v = _MultiView(bufs[0], ap)
    v.bufs = bufs
    return v


_orig_dma = Sched.dma


def _dma(self, q, out, in_):
    if isinstance(out, _MultiView):
        extra = out.bufs[1:]
        self._deps(q, [], extra)
        _orig_dma(self, q, View(out.bufs[0], out.ap), in_)
        tick = out.bufs[0].last_write
        for b in extra:
            b.last_write = tick
            b.readers = {}
    else:
        _orig_dma(self, q, out, in_)


Sched.dma = _dma


def _cols(v):
    v = np.asarray(v, np.float32)
    return np.ascontiguousarray(v.reshape(-1, 128).T)


def _consts():
    ident = np.eye(128, dtype=np.float32)
    tri = np.triu(np.ones((128, 128), np.float32))
    mask = np.tile(tri, (1, 4))
    sel = np.zeros((128, 256), np.float32)
    sel[127, 0:128] = 1.0
    sel[63, 128:256] = 1.0
    rc = np.zeros((128, 64), np.float32)
    for g, w in enumerate(WINDOWS):
        for t in range(16):
            rc[:, 16 * g + t] = 1.0 / min(w, t + 1)
    return ident, mask, sel, rc


_PROG_CACHE = {}


def kernel(x_prompt, x_sample, state_pool, state_mlstm_C, state_mlstm_n, state_mlstm_m, state_conv,
           c_prompt, c_sample, w_ada, b_ada, g_norm1, w_in, b_gate, w_pool, s_pool, g_head, w_out,
           g_norm2, w_up, w_conv, b_conv, w_down, g_final, _cfg=None):
    f32 = np.float32
    x_prompt = np.asarray(x_prompt, f32)
    x_sample = np.asarray(x_sample, f32)
    B, T, _ = x_prompt.shape
    BS, TS, _ = x_sample.shape
    depth = w_in.shape[0]
    ncores = B // 2
    assert BS == B
    GT = min(1024, T)
    NB = min(512, GT)
    if _cfg is not None:
        GT, NB = _cfg
    key = (T, GT, NB, TS, depth)
    if key not in _PROG_CACHE:
        _PROG_CACHE[key] = build_program(T, GT, NB, TS, depth)
    nc = _PROG_CACHE[key]

    ident, mask, sel, rc = _consts()
    pvl = []
    for l in range(depth):
        pvl += [_cols(b_ada[l]), _cols(g_norm1[l]), _cols(g_norm2[l]), _cols(s_pool[l]), _cols(g_head[l]),
                np.ascontiguousarray(np.asarray(w_conv[l], f32).reshape(3, NFT, 128).transpose(2, 0, 1).reshape(128, 3 * NFT)),
                _cols(b_conv[l])]
    pvl.append(_cols(g_final))
    pv = np.ascontiguousarray(np.concatenate(pvl, axis=1))
    assert pv.shape == (128, NPV)
    bgv = np.asarray(b_gate, f32)
    bg = np.zeros((4, depth * 2), f32)
    for l in range(depth):
        bg[:, 2 * l] = bgv[l, 0:4]
        bg[:, 2 * l + 1] = bgv[l, 4:8]
    wts = dict(w_ada=np.asarray(w_ada, f32), w_in=np.asarray(w_in, f32), w_pool=np.asarray(w_pool, f32),
               w_out=np.asarray(w_out, f32), w_up=np.asarray(w_up, f32), w_down=np.asarray(w_down, f32))
    state_pool = np.asarray(state_pool, f32)
    state_mlstm_C = np.asarray(state_mlstm_C, f32)
    state_mlstm_n = np.asarray(state_mlstm_n, f32)
    state_mlstm_m = np.asarray(state_mlstm_m, f32)
    state_conv = np.asarray(state_conv, f32)
    c_prompt = np.asarray(c_prompt, f32)
    c_sample = np.asarray(c_sample, f32)

    in_maps = []
    for c in range(ncores):
        bs = slice(2 * c, 2 * c + 2)
        xp = np.ascontiguousarray(x_prompt[bs].reshape(2, T, KC, 128).transpose(0, 3, 2, 1))
        xs = np.ascontiguousarray(x_sample[bs].reshape(2 * TS, KC, 128).transpose(2, 1, 0))
        cc = np.concatenate([c_prompt[bs], c_sample[bs]], 0)
        cT = np.ascontiguousarray(cc.reshape(4, KC, 128).transpose(2, 1, 0))
        hist = np.ascontiguousarray(state_pool[:, bs].reshape(depth, 2, 15, 4, 128).transpose(4, 0, 1, 3, 2))
        Ct = state_mlstm_C[:, bs].transpose(4, 0, 1, 2, 3)
        nt = state_mlstm_n[:, bs].transpose(3, 0, 1, 2)[..., None]
        spC = np.ascontiguousarray(np.concatenate([Ct, nt], axis=-1))
        mrow = np.ascontiguousarray(state_mlstm_m[:, bs][..., None])
        mbc = np.ascontiguousarray(np.broadcast_to(state_mlstm_m[:, bs][None], (128, depth, 2, NH)))
        conv = np.ascontiguousarray(state_conv[:, bs].reshape(depth, 2, 2, NFT, 128).transpose(4, 0, 1, 3, 2))
        m = dict(xp=xp, xs=xs, cT=cT, pv=pv, bg=bg, ident=ident, mask=mask, sel=sel, rc=rc,
                 sp_hist=hist, sp_C=spC, sp_mrow=mrow, sp_mbc=mbc, sp_conv=conv)
        m.update(wts)
        in_maps.append(m)

    res = run_bass_kernel_spmd(nc, in_maps, core_ids=list(range(ncores)))
    R = res.results

    y_prompt = np.zeros((B, T, D), f32)
    y_sample = np.zeros((BS, TS, D), f32)
    outs = {}
    for nm, nb in (("p", B), ("s", BS)):
        outs["pool" + nm] = np.zeros((depth, nb, 15, DP), f32)
        outs["C" + nm] = np.zeros((depth, nb, NH, HD, HD), f32)
        outs["n" + nm] = np.zeros((depth, nb, NH, HD), f32)
        outs["m" + nm] = np.zeros((depth, nb, NH), f32)
        outs["conv" + nm] = np.zeros((depth, nb, 2, 2 * DFF), f32)
    for c in range(ncores):
        r = R[c]
        ypc = np.asarray(r["yp"])
        y_prompt[2 * c:2 * c + 2] = ypc.transpose(0, 3, 2, 1).reshape(2, T, D)
        ysc = np.asarray(r["ys"])
        y_sample[2 * c:2 * c + 2] = ysc.transpose(2, 1, 0).reshape(2, TS, D)
        op_ = np.asarray(r["o_pool"])
        oC = np.asarray(r["o_C"])
        om = np.asarray(r["o_m"])[..., 0]
        oc = np.asarray(r["o_conv"])
        for q in range(4):
            nm = "p" if q < 2 else "s"
            b = 2 * c + (q % 2)
            outs["pool" + nm][:, b] = op_[:, :, q].transpose(1, 3, 2, 0).reshape(depth, 15, DP)
            outs["C" + nm][:, b] = oC[:, :, q, :, 0:128].transpose(1, 2, 3, 0)
            outs["n" + nm][:, b] = oC[:, :, q, :, 128].transpose(1, 2, 0)
            outs["m" + nm][:, b] = om[:, q, :]
            outs["conv" + nm][:, b] = oc[:, :, q].transpose(1, 3, 2, 0).reshape(depth, 2, 2 * DFF)
    return (y_prompt, y_sample,
            outs["poolp"], outs["Cp"], outs["np"], outs["mp"], outs["convp"],
            outs["pools"], outs["Cs"], outs["ns"], outs["ms"], outs["convs"])
```

```python
import numpy as np
from contextlib import ExitStack
import concourse.bass as bass
import concourse.mybir as mybir
from concourse.bass_utils import run_bass_kernel_spmd

F32 = mybir.dt.float32
BF16 = mybir.dt.bfloat16
AF = mybir.ActivationFunctionType
ALU = mybir.AluOpType

D = 1024
KC = 8
DP = 512
NH = 4
HD = 128
DIN = 2568
DFF = 2816
NFT = 44
NKT = 22
EPS = 1e-6
PVL = 248
NPV = 2 * PVL + 8
WINDOWS = (2, 4, 8, 16)


class View:
    def __init__(self, buf, ap):
        self.buf = buf
        self.ap = ap

    def __getitem__(self, idx):
        return View(self.buf, self.ap[idx])

    def rearrange(self, s, **kw):
        return View(self.buf, self.ap.rearrange(s, **kw))

    def bcast(self, shape):
        return View(self.buf, self.ap.broadcast_to(list(shape)))

    def bitcast(self, dt):
        return View(self.buf, self.ap.bitcast(dt))

    def also(self, *bufs):
        v = View(self.buf, self.ap)
        v.bufs = [self.buf] + [b.buf for b in bufs]
        return v


class Buf(View):
    def __init__(self, name, ap):
        self.name = name
        self.buf = self
        self.ap = ap
        self.last_write = None
        self.readers = {}


WRITE_KEYS = ("out", "accum_out", "ap")
COMPUTE = ("pe", "act", "dve", "pool")
ALLENG = ("pe", "act", "dve", "pool", "sp")


class Sched:
    def __init__(self, nc, es):
        self.nc = nc
        self.es = es
        self.plan = {e: [] for e in ALLENG}
        self.count = {e: 0 for e in ALLENG}
        self.sem = {e: es.enter_context(nc.semaphore("S_" + e)) for e in ALLENG}
        self.seen = {e: {} for e in ALLENG}
        self.dma_sems = {}
        self.dma_cnt = {}
        self.dma_rr = {}
        for q, n in (("sp", 12), ("pool", 12)):
            self.dma_sems[q] = [es.enter_context(nc.semaphore(f"D_{q}{i}")) for i in range(n)]
            self.dma_cnt[q] = [0] * n
            self.dma_rr[q] = 0

    def _wait(self, eng, tick):
        if tick is None:
            return
        kind, key, val = tick
        if kind == "eng" and key == eng and eng == "pe":
            return
        k = (kind, key)
        if self.seen[eng].get(k, 0) >= val:
            return
        self.seen[eng][k] = val
        sem = self.sem[key] if kind == "eng" else self.dma_sems[key[0]][key[1]]
        self.plan[eng].append(("wait", sem, val))

    def _deps(self, eng, reads, writes):
        for b in reads:
            self._wait(eng, b.last_write)
            if getattr(b, "excl", False):
                for (kind, key), t in b.readers.items():
                    if not (kind == "eng" and key == eng):
                        self._wait(eng, t)
        for b in writes:
            self._wait(eng, b.last_write)
            for t in b.readers.values():
                self._wait(eng, t)

    def _commit(self, tick, reads, writes):
        for b in writes:
            b.last_write = tick
            b.readers = {}
        for b in reads:
            if b in writes:
                continue
            b.readers[(tick[0], tick[1])] = tick

    def op(self, eng, method, signal=True, **kw):
        reads, writes, real = [], [], {}
        for k, v in kw.items():
            if isinstance(v, View):
                for bb in getattr(v, "bufs", None) or [v.buf]:
                    (writes if k in WRITE_KEYS else reads).append(bb)
                real[k] = v.ap
            else:
                real[k] = v
        self._deps(eng, reads, writes)
        if signal:
            self.count[eng] += 1
            tick = ("eng", eng, self.count[eng])
        else:
            tick = ("eng", eng, self.count[eng] + 1)
        assert self.count[eng] < 60000
        self.plan[eng].append(("inst", method, real, signal))
        self._commit(tick, reads, writes)

    def dma(self, q, out, in_):
        reads, writes = [], []
        o_ap, i_ap = out, in_
        if isinstance(out, View):
            writes.append(out.buf)
            o_ap = out.ap
        if isinstance(in_, View):
            reads.append(in_.buf)
            i_ap = in_.ap
        self._deps(q, reads, writes)
        n = len(self.dma_sems[q])
        i = self.dma_rr[q]
        self.dma_rr[q] = (i + 1) % n
        self._wait(q, ("dma", (q, i), self.dma_cnt[q][i]))
        self.dma_cnt[q][i] += 16
        tick = ("dma", (q, i), self.dma_cnt[q][i])
        self.plan[q].append(("dma", o_ap, i_ap, self.dma_sems[q][i]))
        self._commit(tick, reads, writes)

    def barrier(self):
        for q in ("sp", "pool"):
            for i in range(len(self.dma_sems[q])):
                if self.dma_cnt[q][i] > 0:
                    self._wait(q, ("dma", (q, i), self.dma_cnt[q][i]))
        for q in ("sp",):
            self.count[q] += 1
            self.plan[q].append(("seminc", self.sem[q]))
        self.op("pool", "memset", ap=self.tick_buf, constant=0.0)
        snap = dict(self.count)
        for e in ALLENG:
            for e2 in ALLENG:
                if e2 != e and snap[e2] > 0:
                    self._wait(e, ("eng", e2, snap[e2]))

    def replay(self, block):
        plan = self.plan

        def run(e, items):
            for it in items:
                if it[0] == "wait":
                    e.wait_ge(it[1], it[2])
                elif it[0] == "inst":
                    ins = getattr(e, it[1])(**it[2])
                    if it[3]:
                        ins.then_inc(self.sem[self._name_of(e)], 1)
                elif it[0] == "dma":
                    e.dma_start(out=it[1], in_=it[2]).then_inc(it[3], 16)
                elif it[0] == "seminc":
                    e.sem_inc(it[1], 1)

        self._names = {}

        @block.tensor
        def _(e):
            self._names[id(e)] = "pe"
            run(e, plan["pe"])

        @block.scalar
        def _(e):
            self._names[id(e)] = "act"
            run(e, plan["act"])

        @block.vector
        def _(e):
            self._names[id(e)] = "dve"
            run(e, plan["dve"])

        @block.gpsimd
        def _(e):
            self._names[id(e)] = "pool"
            run(e, plan["pool"])

        @block.sync
        def _(e):
            self._names[id(e)] = "sp"
            run(e, plan["sp"])

    def _name_of(self, e):
        return self._names[id(e)]


def build_program(T, GT, NB, TS=64, depth=2):
    assert T % GT == 0 and GT % NB == 0 and NB % 128 == 0 and NB <= 512
    nc = bass.Bass("TRN2", target_bir_lowering=False)
    es = ExitStack()
    NBLK = GT // NB
    NSEQ = 4
    SB = 2 * TS

    def din(name, shape, dt=F32):
        return nc.dram_tensor(name, list(shape), dt, kind="ExternalInput").ap()

    def dout(name, shape, dt=F32):
        return nc.dram_tensor(name, list(shape), dt, kind="ExternalOutput").ap()

    xp = din("xp", [2, 128, KC, T])
    xs = din("xs", [128, KC, SB])
    cT_d = din("cT", [128, KC, NSEQ])
    w_ada = din("w_ada", [depth, D, 6 * D])
    w_in = din("w_in", [depth, D, DIN])
    w_pool = din("w_pool", [depth, 4, 128, 128])
    w_out = din("w_out", [depth, D, D])
    w_up = din("w_up", [depth, D, 2 * DFF])
    w_down = din("w_down", [depth, DFF, D])
    pv_d = din("pv", [128, NPV])
    bg_d = din("bg", [4, depth * 2])
    ident_d = din("ident", [128, 128])
    mask_d = din("mask", [128, 512])
    sel_d = din("sel", [128, 256])
    rc_d = din("rc", [128, 64])
    sp_hist = din("sp_hist", [128, depth, 2, 4, 15])
    sp_C = din("sp_C", [128, depth, 2, NH, 129])
    sp_mrow = din("sp_mrow", [depth, 2, 4, 1])
    sp_mbc = din("sp_mbc", [128, depth, 2, NH])
    sp_conv = din("sp_conv", [128, depth, 2, NFT, 2])
    yp = dout("yp", [2, 128, KC, T])
    ys = dout("ys", [128, KC, SB])
    o_pool = dout("o_pool", [128, depth, NSEQ, 4, 15])
    o_C = dout("o_C", [128, depth, NSEQ, NH, 129])
    o_m = dout("o_m", [depth, NSEQ, 4, 1])
    o_conv = dout("o_conv", [128, depth, NSEQ, NFT, 2])

    S = Sched(nc, es)
    S.tick_buf = Buf("pooltick", es.enter_context(nc.sbuf_tensor("pooltick", [128, 1], F32))[:])

    def sbt(name, shape, dt=F32):
        t = es.enter_context(nc.sbuf_tensor(name, list(shape), dt))
        return t

    def sbuf(name, shape, dt=F32):
        t = sbt(name, shape, dt)
        return Buf(name, t[:])

    SBLK = NBLK
    xT_t = sbt("xT", [128, KC, GT + SB])
    xT = [[Buf(f"xT{kc}_{b}", xT_t[:, kc, b * NB:(b + 1) * NB]) for b in range(NBLK)] +
          [Buf(f"xT{kc}_s", xT_t[:, kc, GT:GT + SB])] for kc in range(KC)]
    xT_all = [xT[kc][b] for kc in range(KC) for b in range(NBLK)]
    hT_t = [sbt(f"hT{i}", [128, KC, NB], BF16) for i in range(2)]
    hT = [[Buf(f"hT{i}_{kc}", hT_t[i][:, kc, :]) for kc in range(KC)] for i in range(2)]
    hTs_t = sbt("hTs", [128, KC, SB], BF16)
    hT.append([Buf(f"hTs_{kc}", hTs_t[:, kc, :]) for kc in range(KC)])
    sq = hT[1]
    rstd = sbuf("rstd", [128, NB])
    ntmp = [sbuf(f"ntmp{i}", [128, NB]) for i in range(2)]
    pv = sbuf("pvs", [128, NPV])
    ident = sbuf("ident_s", [128, 128])
    identb = sbuf("identb", [128, 128], BF16)
    onesb = sbuf("onesb", [128, 128], BF16)
    mask = sbuf("mask_s", [128, 512], BF16)
    mask_f = None
    sel = sbuf("sel_s", [128, 256])
    rc = sbuf("rc_s", [128, 64])
    bg = sbuf("bg_s", [4, depth * 2])
    nbf = sbuf("nbf", [4, depth])
    ones4c = sbuf("ones4c", [4, 1])
    cTs = sbuf("cTs", [128, KC, NSEQ])
    scT = sbuf("scT", [128, KC, NSEQ], BF16)
    mod = [sbuf(f"mod{l}", [128, 48, NSEQ]) for l in range(depth)]
    A1 = [sbuf(f"A1_{l}", [128, KC, NSEQ]) for l in range(depth)]
    A2 = [sbuf(f"A2_{l}", [128, KC, NSEQ]) for l in range(depth)]
    zero_col = sbuf("zero_col", [128, 1])
    junk = sbuf("junk", [128, 128], BF16)

    class St:
        pass
    states = {}
    for slot in range(3):
        for l in range(depth):
            st = St()
            n = f"{slot}_{l}"
            st.hist = sbuf("hist" + n, [128, 4, 15])
            st.Dm = sbuf("Dm" + n, [128, NH, 129])
            st.Cbf = sbuf("Cbf" + n, [128, NH, 129], BF16)
            st.glast = sbuf("glast" + n, [128, NH])
            st.Eprev = sbuf("Eprev" + n, [128, NH])
            st.Bprev = sbuf("Bprev" + n, [4, 1])
            st.mprev = sbuf("mprev" + n, [4, 1])
            st.cc = sbuf("cc" + n, [128, NFT, 2])
            states[(slot, l)] = st

    class Sm:
        pass
    smalls = []
    for i in range(3):
        sm = Sm()
        sm.dn = sbuf(f"dn_{i}", [128, 4])
        sm.a1 = sbuf(f"a1_{i}", [128, 4])
        sm.scale = sbuf(f"scale_{i}", [128, 4])
        sm.ssq = sbuf(f"ssq_{i}", [128, 4])
        sm.t2 = sbuf(f"t2_{i}", [128, 4])
        sm.comb = sbuf(f"comb_{i}", [128, 4])
        smalls.append(sm)

    NTMAX = max(NB // 128, 2)

    class Bs:
        pass
    bsets = []
    for i in range(2):
        bs = Bs()
        bs.tok = sbuf(f"btok{i}", [128, NTMAX, 12])
        bs.epv = sbuf(f"bepv{i}", [128, NTMAX, 4])
        bs.t1 = sbuf(f"bt1{i}", [128, NTMAX, 24])
        bs.ex = sbuf(f"bex{i}", [128, NTMAX, 24])
        bsets.append(bs)
    blk_ctr = [0]

    used = 208832 - nc.sbuf_bytes_remaining
    ARENA_BYTES = 114 * 1024
    arena = sbt("arena", [128, ARENA_BYTES // 4])

    class Layout:
        def __init__(self):
            self.off = 0

        def alloc(self, name, shape, dt=F32):
            esz = 4 if dt == F32 else 2
            n = int(np.prod(shape[1:])) * esz
            n4 = (n + 3) // 4
            assert self.off + n4 <= ARENA_BYTES // 4, (name, self.off * 4, n)
            ap = arena[:, self.off:self.off + n4]
            if dt != F32:
                ap = ap.bitcast(dt)
            ne = int(np.prod(shape[1:]))
            ap = ap[:, 0:ne]
            if len(shape) == 3:
                ap = ap.rearrange("p (a b) -> p a b", a=shape[1])
            elif len(shape) == 4:
                ap = ap.rearrange("p (a b c) -> p a b c", a=shape[1], b=shape[2])
            self.off += n4
            if shape[0] < 128:
                ap = ap[0:shape[0]]
            return Buf(name, ap)

    L0 = Layout()
    wada_s = [L0.alloc(f"wada{i}", [128, KC, 1024], BF16) for i in range(2)]

    L1 = Layout()
    wi_u = L1.alloc("wi_u", [128, KC, 512], BF16)
    wi_q = L1.alloc("wi_q", [128, KC, 512], BF16)
    wi_k = L1.alloc("wi_k", [128, KC, 512], BF16)
    wi_vo = L1.alloc("wi_vo", [128, KC, 1024], BF16)
    wi_g = L1.alloc("wi_g", [128, KC, 8], BF16)
    wo_s = L1.alloc("wo_s", [128, KC, 1024], BF16)
    wp_s = L1.alloc("wp_s", [128, 4, 128], BF16)
    qT = [L1.alloc(f"qT{h}", [128, NB], BF16) for h in range(NH)]
    kT = [L1.alloc(f"kT{h}", [128, NB], BF16) for h in range(NH)]
    UT = [[L1.alloc(f"UT{s}_{g}", [128, 15 + (NB if s == 0 else TS)]) for g in range(4)] for s in range(2)]
    ptmp = [L1.alloc(f"ptmp{i}", [128, 15 + NB]) for i in range(2)]
    pfix = L1.alloc("pfix", [128, 16])
    diffT = [L1.alloc(f"diffT{g}", [128, NB], BF16) for g in range(4)]
    mixT = [L1.alloc(f"mixT{k}", [128, NB], BF16) for k in range(KC)]
    igs = L1.alloc("igs", [4, NB])
    lfs = L1.alloc("lfs", [4, NB])
    e1s = lfs
    Brow = L1.alloc("Brow", [4, NB])
    Mrow = L1.alloc("Mrow", [4, NB])
    Erow = Brow
    Grow = igs
    vaug = [L1.alloc(f"vaug{i}", [128, NH, 129], BF16) for i in range(3)]
    ktok = [L1.alloc(f"ktok{i}", [128, 512], BF16) for i in range(3)]
    go = [L1.alloc(f"go{i}", [128, 512]) for i in range(3)]
    SmT = [L1.alloc(f"SmT{i}", [128, NH, 128], BF16) for i in range(2)]
    hs = [L1.alloc("hs0", [128, 512], BF16)] * 2

    L2 = Layout()
    actT = [[L2.alloc(f"actT{j}_{b}", [128, NB], BF16) for b in range(NBLK)] + [L2.alloc(f"actT{j}_s", [128, SB], BF16)]
            for j in range(NKT)]
    wup_s = [L2.alloc(f"wup{i}", [128, KC, 512], BF16) for i in range(2)]
    wdn_s = [L2.alloc(f"wdn{i}", [128, NKT, 128], BF16) for i in range(3)]
    acc = [L2.alloc(f"acc{i}", [128, NB]) for i in range(6)]
    sgt = [L2.alloc(f"sg{i}", [128, NB]) for i in range(2)]
    UPt = [L2.alloc(f"UP{i}", [128, NB + 4]) for i in range(4)]
    UPp = [Buf(f"UPp{i}", UPt[i].ap) for i in range(4)]
    NACC = 6

    L3 = Layout()
    yT = [L3.alloc(f"yT{i}", [128, KC, NB]) for i in range(2)]
    stC = L3.alloc("stC", [128, NH, 129])

    PS = []
    for i in range(8):
        t = es.enter_context(nc.psum_tensor(f"ps{i}", [128, 512], F32))
        PS.append(Buf(f"ps{i}", t[:]))
        PS[-1].excl = True

    def pvcol(l, off, n=1):
        base = l * PVL + off
        return pv[:, base:base + n]

    OFF_BADA, OFF_G1, OFF_G2, OFF_SPOOL, OFF_GHEAD, OFF_WCONV, OFF_BCONV = 0, 48, 56, 64, 68, 72, 204

    fm_rr = [0]

    def fm_bank(banks):
        b = banks[fm_rr[0] % len(banks)]
        fm_rr[0] += 1
        return b

    S.dma("sp", pv, pv_d)
    S.dma("sp", ident, ident_d)
    S.dma("pool", mask, mask_d)
    S.dma("sp", sel, sel_d)
    S.dma("sp", rc, rc_d)
    S.dma("sp", bg, bg_d)
    S.dma("sp", cTs, cT_d)
    S.op("dve", "tensor_copy", out=identb, in_=ident)
    S.op("dve", "memset", ap=onesb, constant=1.0)
    S.op("dve", "memset", ap=ones4c, constant=1.0)
    S.op("dve", "memset", ap=zero_col, constant=0.0)
    for l in range(depth):
        S.op("dve", "tensor_scalar", out=nbf[:, l:l + 1], in0=bg[:, 2 * l + 1:2 * l + 2], scalar1=-1.0,
             scalar2=None, op0=ALU.mult)
    S.op("act", "activation", out=scT, in_=cTs, func=AF.Silu)

    for l in range(depth):
        wv = w_ada[l].rearrange("(kc p) n -> p kc n", p=128)
        for piece in range(6):
            wb = wada_s[piece % 2]
            S.dma("pool", wb, wv[:, :, piece * 1024:(piece + 1) * 1024])
            for fcl in range(8):
                fc = piece * 8 + fcl
                for kc in range(KC):
                    S.op("pe", "matmul", signal=(kc == KC - 1), out=PS[0][:, fc * 4:fc * 4 + 4],
                         lhsT=wb[:, kc, fcl * 128:(fcl + 1) * 128], rhs=scT[:, kc, :],
                         start=(kc == 0), stop=(kc == KC - 1))
        S.op("dve", "tensor_tensor", out=mod[l], in0=PS[0][:, 0:192].rearrange("p (a b) -> p a b", b=NSEQ),
             in1=pvcol(l, OFF_BADA, 48).rearrange("p (a b) -> p a b", b=1).bcast([128, 48, NSEQ]), op=ALU.add)
        for (A, goff, scj) in ((A1[l], OFF_G1, 1), (A2[l], OFF_G2, 4)):
            S.op("dve", "tensor_scalar", out=A, in0=mod[l][:, scj * 8:(scj + 1) * 8, :], scalar1=1.0, scalar2=None,
                 op0=ALU.add)
            S.op("dve", "tensor_tensor", out=A, in0=A,
                 in1=pvcol(l, goff, 8).rearrange("p (a b) -> p a b", b=1).bcast([128, 8, NSEQ]), op=ALU.mult)
    S.barrier()

    def norm(blk, ncols, segs, scale_fn, bias_fn, outs):
        for kc in range(KC):
            if kc % 2 == 0:
                S.op("act", "activation", out=sq[kc][:, 0:ncols], in_=xT[kc][blk][:, 0:ncols], func=AF.Square)
            else:
                S.op("dve", "tensor_tensor", out=sq[kc][:, 0:ncols], in0=xT[kc][blk][:, 0:ncols],
                     in1=xT[kc][blk][:, 0:ncols], op=ALU.mult)
        for kc in range(KC):
            S.op("pe", "matmul", signal=(kc == KC - 1), out=PS[0][:, 0:ncols], lhsT=onesb, rhs=sq[kc][:, 0:ncols],
                 start=(kc == 0), stop=(kc == KC - 1))
        S.op("act", "activation", out=rstd[:, 0:ncols], in_=PS[0][:, 0:ncols], func=AF.Ln, scale=1.0 / D, bias=EPS)
        S.op("act", "activation", out=rstd[:, 0:ncols], in_=rstd[:, 0:ncols], func=AF.Exp, scale=-0.5)
        for kc in range(KC):
            nt = ntmp[kc % 2]
            S.op("dve", "tensor_tensor", out=nt[:, 0:ncols], in0=xT[kc][blk][:, 0:ncols], in1=rstd[:, 0:ncols],
                 op=ALU.mult)
            for (c0, c1, q) in segs:
                S.op("act", "activation", out=outs[kc][:, c0:c1], in_=nt[:, c0:c1], func=AF.Identity,
                     scale=scale_fn(kc, q), bias=bias_fn(kc, q))

    def load_w1(l):
        wv = w_in[l].rearrange("(kc p) n -> p kc n", p=128)
        S.dma("pool", wi_g, wv[:, :, 2560:2568])
        S.dma("pool", wi_k, wv[:, :, 1024:1536])
        S.dma("pool", wi_q, wv[:, :, 512:1024])
        S.dma("pool", wi_u, wv[:, :, 0:512])
        S.dma("pool", wi_vo, wv[:, :, 1536:2560])
        S.dma("pool", wp_s, w_pool[l].rearrange("g c d -> c g d"))
        S.dma("pool", wo_s, w_out[l].rearrange("(kc p) n -> p kc n", p=128))

    def phase1_block(l, blk, ncols, segs, tiles, first_block):
        h = hT[0]
        norm(blk, ncols, [(c0, c1, q) for (c0, c1, q, _) in segs],
             lambda kc, q: A1[l][:, kc, q:q + 1], lambda kc, q: mod[l][:, 0 * 8 + kc, q:q + 1], h)
        for (ps, c0) in ((PS[3], 0), (PS[4], 4)):
            for kc in range(KC):
                S.op("pe", "matmul", signal=(kc == KC - 1), out=ps[0:4, 0:ncols], lhsT=wi_g[:, kc, c0:c0 + 4],
                     rhs=h[kc][:, 0:ncols], start=(kc == 0), stop=(kc == KC - 1))
        S.op("act", "activation", out=igs[:, 0:ncols], in_=PS[3][0:4, 0:ncols], func=AF.Identity,
             bias=bg[:, 2 * l:2 * l + 1], scale=1.0)
        S.op("act", "activation", out=e1s[:, 0:ncols], in_=PS[4][0:4, 0:ncols], func=AF.Exp,
             bias=nbf[:, l:l + 1], scale=-1.0)
        S.op("act", "activation", out=e1s[:, 0:ncols], in_=e1s[:, 0:ncols], func=AF.Ln, bias=1.0, scale=1.0)
        S.op("dve", "tensor_scalar", out=lfs[:, 0:ncols], in0=e1s[:, 0:ncols], scalar1=-1.0, scalar2=None,
             op0=ALU.mult)
        for (c0, c1, q, slot) in segs:
            st = states[(slot, l)]
            S.op("dve", "tensor_tensor_scan", out=Brow[:, c0:c1], data0=ones4c.bcast([4, c1 - c0]), data1=lfs[:, c0:c1],
                 initial=st.Bprev[:, 0:1], op0=ALU.mult, op1=ALU.add)
            S.op("dve", "tensor_tensor_scan", out=Mrow[:, c0:c1], data0=lfs[:, c0:c1], data1=igs[:, c0:c1],
                 initial=st.mprev[:, 0:1], op0=ALU.add, op1=ALU.max)
            S.op("dve", "tensor_copy", out=st.Bprev, in_=Brow[:, c1 - 1:c1])
            S.op("dve", "tensor_copy", out=st.mprev, in_=Mrow[:, c1 - 1:c1])
        S.op("dve", "tensor_tensor", out=Grow[:, 0:ncols], in0=igs[:, 0:ncols], in1=Brow[:, 0:ncols], op=ALU.subtract)
        S.op("dve", "tensor_tensor", out=Erow[:, 0:ncols], in0=Brow[:, 0:ncols], in1=Mrow[:, 0:ncols], op=ALU.subtract)

        def proj_fm(wbuf, m):
            ps = fm_bank([PS[1], PS[2]])
            for kc in range(KC):
                S.op("pe", "matmul", signal=(kc == KC - 1), out=ps[:, 0:ncols], lhsT=wbuf[:, kc, m * 128:(m + 1) * 128],
                     rhs=h[kc][:, 0:ncols], start=(kc == 0), stop=(kc == KC - 1))
            return ps
        for hh in range(NH):
            ps = proj_fm(wi_k, hh)
            S.op("act", "activation", out=kT[hh][:, 0:ncols], in_=ps[:, 0:ncols], func=AF.Copy, scale=float(HD ** -0.5))
        for hh in range(NH):
            ps = proj_fm(wi_q, hh)
            S.op("act", "activation", out=qT[hh][:, 0:ncols], in_=ps[:, 0:ncols], func=AF.Copy)
        for g in range(4):
            ps = proj_fm(wi_u, g)
            for si, (c0, c1, q, slot) in enumerate(segs):
                S.op("act", "activation", out=UT[si][g][:, 15:15 + (c1 - c0)], in_=ps[:, c0:c1], func=AF.Copy)

        for si, (c0, c1, q, slot) in enumerate(segs):
            st = states[(slot, l)]
            n = c1 - c0
            W = 15 + n
            for g in range(4):
                S.op("pool", "tensor_copy", out=UT[si][g][:, 0:15], in_=st.hist[:, g, :])
            for g, w in enumerate(WINDOWS):
                U = UT[si][g]
                nlev = {2: 1, 4: 2, 8: 3, 16: 4}[w]
                starts = [0] * (nlev + 1)
                starts[nlev] = 15
                for j in range(nlev, 1, -1):
                    starts[j - 1] = starts[j] - 2 ** (j - 1)
                src = U
                for j in range(1, nlev + 1):
                    dst = ptmp[j % 2]
                    s0 = starts[j]
                    sh = 2 ** (j - 1)
                    S.op("pool", "tensor_tensor", out=dst[:, s0:W], in0=src[:, s0:W], in1=src[:, s0 - sh:W - sh], op=ALU.add)
                    src = dst
                other = ptmp[(nlev + 1) % 2]
                S.op("pool", "tensor_scalar", out=other[:, 15:W], in0=src[:, 15:W], scalar1=1.0 / w, scalar2=0.0,
                     op0=ALU.mult, op1=ALU.add)
                S.op("pool", "tensor_tensor", out=diffT[g][:, c0:c1], in0=other[:, 15:W], in1=U[:, 15:W], op=ALU.subtract)
                if first_block:
                    S.op("pool", "tensor_tensor", out=pfix, in0=src[:, 15:31], in1=rc[:, 16 * g:16 * g + 16], op=ALU.mult)
                    S.op("pool", "tensor_tensor", out=diffT[g][:, c0:c0 + 16], in0=pfix, in1=U[:, 15:31], op=ALU.subtract)
                S.op("pool", "tensor_copy", out=st.hist[:, g, :], in_=U[:, W - 15:W])

        Pbanks = [(PS[3], PS[4]), (PS[5], PS[6])]
        RB = [PS[0], PS[1], PS[2], PS[7]]
        bs = bsets[blk_ctr[0] % 2]
        blk_ctr[0] += 1
        nt = len(tiles)
        LL = tiles[0][1]
        selL = sel[0:LL, 0:128] if LL == 128 else sel[0:LL, 128:256]
        EB0 = 12 * NTMAX
        for ti, (c0, L, q, slot) in enumerate(tiles):
            for j, row in enumerate((Erow, Grow, Mrow)):
                S.op("pe", "transpose", signal=(ti == nt - 1 and j == 2), out=PS[0][0:L, 12 * ti + 4 * j:12 * ti + 4 * j + 4],
                     in_=row[:, c0:c0 + L], identity=ident[0:4, 0:4])
        S.op("dve", "tensor_copy", out=bs.tok[0:LL, 0:nt, :],
             in_=PS[0][0:LL, 0:12 * nt].rearrange("p (a b) -> p a b", b=12))
        for ti in range(nt):
            S.op("pe", "matmul", signal=(ti == nt - 1), out=PS[0][:, EB0 + 4 * ti:EB0 + 4 * ti + 4], lhsT=selL,
                 rhs=bs.tok[0:LL, ti, 0:4], start=True, stop=True)
        ebc = PS[0][:, EB0:EB0 + 4 * nt].rearrange("p (a b) -> p a b", b=4)
        prev_of = {}
        last_of = {}
        for ti, (c0, L, q, slot) in enumerate(tiles):
            prev_of[ti] = last_of.get(slot)
            last_of[slot] = ti
        ti = 0
        while ti < nt:
            if prev_of[ti] is None:
                S.op("dve", "tensor_copy", out=bs.epv[:, ti, :], in_=states[(tiles[ti][3], l)].Eprev)
                ti += 1
            else:
                t_end = ti
                while t_end < nt and prev_of[t_end] == t_end - 1:
                    t_end += 1
                S.op("dve", "tensor_copy", out=bs.epv[:, ti:t_end, :], in_=ebc[:, ti - 1:t_end - 1, :])
                ti = t_end
        S.op("dve", "tensor_tensor", out=bs.t1[0:LL, 0:nt, 0:4], in0=bs.tok[0:LL, 0:nt, 4:8], in1=bs.epv[0:LL, 0:nt, :],
             op=ALU.add)
        S.op("dve", "tensor_tensor", out=bs.t1[0:LL, 0:nt, 4:8], in0=bs.tok[0:LL, 0:nt, 0:4], in1=bs.epv[0:LL, 0:nt, :],
             op=ALU.subtract)
        S.op("dve", "scalar_tensor_tensor", out=bs.t1[0:LL, 0:nt, 8:12], in0=bs.t1[0:LL, 0:nt, 4:8], scalar=-1.0,
             in1=bs.tok[0:LL, 0:nt, 8:12], op0=ALU.mult, op1=ALU.subtract)
        S.op("dve", "tensor_scalar", out=bs.t1[0:LL, 0:nt, 12:16], in0=bs.t1[0:LL, 0:nt, 8:12], scalar1=2.0, scalar2=None,
             op0=ALU.mult)
        S.op("dve", "tensor_scalar", out=bs.t1[0:LL, 0:nt, 16:20], in0=bs.t1[0:LL, 0:nt, 4:8], scalar1=2.0, scalar2=None,
             op0=ALU.mult)
        S.op("dve", "tensor_tensor", out=bs.t1[:, 0:nt, 20:24], in0=ebc, in1=bs.epv[:, 0:nt, :], op=ALU.subtract)
        S.op("act", "activation", out=bs.ex[0:LL, 0:nt, 0:20], in_=bs.t1[0:LL, 0:nt, 0:20], func=AF.Exp)
        S.op("act", "activation", out=bs.ex[:, 0:nt, 20:24], in_=bs.t1[:, 0:nt, 20:24], func=AF.Exp)
        for slot, ti in last_of.items():
            S.op("dve", "tensor_copy", out=states[(slot, l)].Eprev, in_=ebc[:, ti, :])

        def stageA(ti):
            (c0, L, q, slot) = tiles[ti]
            st = states[(slot, l)]
            sm = smalls[ti % 3]
            p2, p3 = ti % 2, ti % 3
            cs = slice(c0, c0 + L)
            u_ = bs.ex[0:L, ti, 0:4]
            va = vaug[p3]
            psv = fm_bank(RB)
            for kc in range(KC):
                S.op("pe", "matmul", signal=(kc == KC - 1), out=psv[0:L, :], lhsT=h[kc][:, cs], rhs=wi_vo[:, kc, 0:512],
                     start=(kc == 0), stop=(kc == KC - 1))
            S.op("dve", "tensor_tensor", out=va[0:L, :, 0:128], in0=psv[0:L, :].rearrange("p (a b) -> p a b", a=NH),
                 in1=u_.rearrange("p (a b) -> p a b", b=1).bcast([L, NH, 128]), op=ALU.mult)
            S.op("dve", "tensor_copy", out=va[0:L, :, 128:129], in_=u_.rearrange("p (a b) -> p a b", b=1))
            pso = fm_bank(RB)
            for kc in range(KC):
                S.op("pe", "matmul", signal=(kc == KC - 1), out=pso[0:L, :], lhsT=h[kc][:, cs], rhs=wi_vo[:, kc, 512:1024],
                     start=(kc == 0), stop=(kc == KC - 1))
            S.op("act", "activation", out=go[p3][0:L, :], in_=pso[0:L, :], func=AF.Exp, scale=-1.0)
            S.op("act", "activation", out=go[p3][0:L, :], in_=go[p3][0:L, :], func=AF.Ln, bias=1.0, scale=1.0)
            S.op("act", "activation", out=go[p3][0:L, :], in_=go[p3][0:L, :], func=AF.Exp, scale=-1.0)
            pst = fm_bank(RB)
            pstb = pst.bitcast(BF16)
            for hh in range(NH):
                S.op("pe", "transpose", signal=(hh == NH - 1), out=pstb[0:L, hh * 128:(hh + 1) * 128], in_=kT[hh][:, cs],
                     identity=identb)
            S.op("act", "activation", out=ktok[p3][0:L, :], in_=pstb[0:L, 0:512], func=AF.Copy)
            pss = fm_bank(RB)
            for hh in range(NH):
                S.op("pe", "matmul", signal=(hh == NH - 1), out=pss[0:L, hh * 128:hh * 128 + L], lhsT=kT[hh][:, cs],
                     rhs=qT[hh][:, cs], start=True, stop=True)
            S.op("dve", "tensor_tensor", out=SmT[p2][0:L, :, 0:L],
                 in0=pss[0:L, :].rearrange("p (a b) -> p a b", a=NH)[:, :, 0:L],
                 in1=mask[0:L, :].rearrange("p (a b) -> p a b", a=NH)[:, :, 0:L], op=ALU.mult)

        def stageP(ti):
            (c0, L, q, slot) = tiles[ti]
            st = states[(slot, l)]
            sm = smalls[ti % 3]
            p2, p3 = ti % 2, ti % 3
            cs = slice(c0, c0 + L)
            va = vaug[p3]
            PB = Pbanks[ti % 2]
            for hh in range(NH):
                ps = PB[hh // 2]
                o0 = (hh % 2) * 129
                S.op("pe", "matmul", signal=False, out=ps[0:L, o0:o0 + 129], lhsT=SmT[p2][0:L, hh, 0:L],
                     rhs=va[0:L, hh, :], start=True, stop=False)
                S.op("pe", "matmul", signal=True, out=ps[0:L, o0:o0 + 129], lhsT=qT[hh][:, cs], rhs=st.Cbf[:, hh, :],
                     start=False, stop=True)

        def stageQ1(ti):
            (c0, L, q, slot) = tiles[ti]
            sm = smalls[ti % 3]
            PB = Pbanks[ti % 2]
            for hh in range(NH):
                ps = PB[hh // 2]
                o0 = (hh % 2) * 129
                S.op("act", "activation", out=junk[0:L, :], in_=ps[0:L, o0:o0 + 128], func=AF.Square, scale=float(HD ** -0.5),
                     accum_out=sm.ssq[0:L, hh:hh + 1])

        def stageQ(ti):
            (c0, L, q, slot) = tiles[ti]
            st = states[(slot, l)]
            sm = smalls[ti % 3]
            p2, p3 = ti % 2, ti % 3
            cs = slice(c0, c0 + L)
            emg2 = bs.ex[0:L, ti, 12:16]
            PB = Pbanks[ti % 2]
            for pi in range(2):
                S.op("dve", "tensor_copy", out=sm.dn[0:L, 2 * pi:2 * pi + 2],
                     in_=PB[pi][0:L, 0:258].rearrange("p (a b) -> p a b", b=129)[:, :, 128])
            S.op("dve", "tensor_tensor", out=sm.a1[0:L, :], in0=sm.dn[0:L, :], in1=sm.dn[0:L, :], op=ALU.mult)
            S.op("dve", "tensor_tensor", out=sm.t2[0:L, :], in0=sm.a1[0:L, :], in1=emg2, op=ALU.max)
            S.op("dve", "scalar_tensor_tensor", out=sm.t2[0:L, :], in0=sm.t2[0:L, :], scalar=EPS, in1=sm.ssq[0:L, :],
                 op0=ALU.mult, op1=ALU.add)
            S.op("dve", "tensor_tensor", out=sm.t2[0:L, :], in0=sm.t2[0:L, :], in1=bs.ex[0:L, ti, 16:20], op=ALU.mult)
            S.op("act", "activation", out=sm.t2[0:L, :], in_=sm.t2[0:L, :], func=AF.Ln)
            S.op("act", "activation", out=sm.scale[0:L, :], in_=sm.t2[0:L, :], func=AF.Exp, scale=-0.5)
            S.op("dve", "tensor_tensor", out=sm.comb[0:L, :], in0=sm.scale[0:L, :], in1=bs.ex[0:L, ti, 4:8], op=ALU.mult)
            for hh in range(NH):
                ps = PB[hh // 2]
                o0 = (hh % 2) * 129
                S.op("dve", "scalar_tensor_tensor", out=hs[p2][0:L, hh * 128:(hh + 1) * 128], in0=ps[0:L, o0:o0 + 128],
                     scalar=sm.comb[0:L, hh:hh + 1], in1=go[p3][0:L, hh * 128:(hh + 1) * 128], op0=ALU.mult, op1=ALU.mult)
            p7b = fm_bank(RB).bitcast(BF16)
            for hh in range(NH):
                S.op("pe", "transpose", signal=(hh == NH - 1), out=p7b[:, hh * 128:hh * 128 + L],
                     in_=hs[p2][0:L, hh * 128:(hh + 1) * 128], identity=identb[0:L, 0:L])
            for hh in range(NH):
                S.op("act", "activation", out=mixT[4 + hh][:, cs], in_=p7b[:, hh * 128:hh * 128 + L], func=AF.Identity,
                     scale=pvcol(l, OFF_GHEAD + hh))

        def stageS(ti):
            (c0, L, q, slot) = tiles[ti]
            st = states[(slot, l)]
            p2, p3 = ti % 2, ti % 3
            va = vaug[p3]
            SB2 = [fm_bank(RB), fm_bank(RB)]
            for hh in range(NH):
                ps = SB2[hh // 2]
                o0 = (hh % 2) * 129
                S.op("pe", "matmul", out=ps[:, o0:o0 + 129], lhsT=ktok[p3][0:L, hh * 128:(hh + 1) * 128],
                     rhs=va[0:L, hh, :], start=True, stop=True)
            gl = bs.ex[:, ti, 20:24].rearrange("p (a b) -> p a b", b=1).bcast([128, NH, 129])
            for pi in range(2):
                S.op("dve", "tensor_tensor", out=st.Dm[:, 2 * pi:2 * pi + 2, :], in0=st.Dm[:, 2 * pi:2 * pi + 2, :],
                     in1=SB2[pi][:, 0:258].rearrange("p (a b) -> p a b", b=129), op=ALU.add)
            S.op("dve", "tensor_tensor", out=st.Cbf, in0=st.Dm, in1=gl, op=ALU.mult)
            S.op("dve", "tensor_tensor", out=st.Dm, in0=st.Dm, in1=gl, op=ALU.mult)

        stageA(0)
        if nt > 1:
            stageA(1)
        for ti in range(nt):
            stageP(ti)
            stageS(ti)
            stageQ1(ti)
            if ti >= 1:
                stageQ(ti - 1)
            if ti + 2 < nt:
                stageA(ti + 2)
        stageQ(nt - 1)

        for g in range(4):
            ps = fm_bank([PS[1], PS[2]])
            S.op("pe", "matmul", out=ps[:, 0:ncols], lhsT=wp_s[:, g, :], rhs=diffT[g][:, 0:ncols], start=True, stop=True)
            S.op("act", "activation", out=mixT[g][:, 0:ncols], in_=ps[:, 0:ncols], func=AF.Identity,
                 scale=pvcol(l, OFF_SPOOL + g))
        for m in range(KC):
            ps = fm_bank([PS[1], PS[2]])
            for kc in range(KC):
                S.op("pe", "matmul", signal=(kc == KC - 1), out=ps[:, 0:ncols], lhsT=wo_s[:, kc, m * 128:(m + 1) * 128],
                     rhs=mixT[kc][:, 0:ncols], start=(kc == 0), stop=(kc == KC - 1))
            for (c0, c1, q, slot) in segs:
                S.op("dve", "scalar_tensor_tensor", out=xT[m][blk][:, c0:c1], in0=ps[:, c0:c1],
                     scalar=mod[l][:, 2 * 8 + m, q:q + 1], in1=xT[m][blk][:, c0:c1], op0=ALU.mult, op1=ALU.add)

    def phase2(l, blocks):
        for (blk, ncols, segs) in sorted(blocks, key=lambda b: (b[0] == 1, b[0])):
            norm(blk, ncols, [(c0, c1, q) for (c0, c1, q, _) in segs],
                 lambda kc, q: A2[l][:, kc, q:q + 1], lambda kc, q: mod[l][:, 3 * 8 + kc, q:q + 1], hT[blk])
        banks = [PS[i] for i in range(1, 8)]
        wv = w_up[l].rearrange("(kc p) (h j c) -> p kc h j c", p=128, h=2, c=128)
        acc_rr = 0
        def load_up(jj):
            wbx = wup_s[jj % 2].rearrange("p k (h j c) -> p k h j c", h=2, j=2)
            for half in range(2):
                S.dma("pool", wbx[:, :, half], wv[:, :, half, 2 * jj:2 * jj + 2, :])
        load_up(0)
        wdv = w_down[l].rearrange("(kt p) n -> p kt n", p=128)
        S.dma("pool", wdn_s[0], wdv[:, :, 0:128])
        S.dma("pool", wdn_s[1], wdv[:, :, 128:256])
        S.dma("pool", wdn_s[2], wdv[:, :, 256:384])
        pending = []

        def back(item):
            (j, blk, ncols, accs) = item
            sg = sgt[(j * len(blocks) + blk) % 2]
            S.op("act", "activation", out=sg[:, 0:ncols], in_=accs[1][:, 0:ncols], func=AF.Silu)
            S.op("dve", "tensor_tensor", out=actT[j][blk][:, 0:ncols], in0=accs[0][:, 0:ncols], in1=sg[:, 0:ncols],
                 op=ALU.mult)

        for jj in range(NKT // 2):
            wb = wup_s[jj % 2]
            wb4 = wb.rearrange("p k (h j c) -> p k h j c", h=2, j=2)
            if jj + 1 < NKT // 2:
                load_up(jj + 1)
            for jl in range(2):
                j = 2 * jj + jl
                for (blk, ncols, segs) in blocks:
                    accs = []
                    for half in range(2):
                        f = half * NKT + j
                        ps = fm_bank(banks)
                        for kc in range(KC):
                            S.op("pe", "matmul", signal=(kc == KC - 1), out=ps[:, 0:ncols], lhsT=wb4[:, kc, half, jl, :],
                                 rhs=hT[blk][kc][:, 0:ncols], start=(kc == 0), stop=(kc == KC - 1))
                        a = acc[acc_rr % NACC]
                        up = UPt[acc_rr % 4]
                        upp = UPp[acc_rr % 4]
                        acc_rr += 1
                        accs.append(a)
                        w0 = pvcol(l, OFF_WCONV + 0 * NFT + f)
                        w1 = pvcol(l, OFF_WCONV + 1 * NFT + f)
                        w2 = pvcol(l, OFF_WCONV + 2 * NFT + f)
                        bc = pvcol(l, OFF_BCONV + f)
                        S.op("act", "activation", out=a[:, 0:ncols], in_=ps[:, 0:ncols], func=AF.Identity, scale=w2, bias=bc)
                        for si, (c0, c1, q, slot) in enumerate(segs):
                            st = states[(slot, l)]
                            off = c0 + 2 * si
                            n = c1 - c0
                            S.op("dve", "tensor_copy", out=View(upp, up.ap[:, off:off + 2]), in_=st.cc[:, f, :])
                            S.op("act", "activation", out=up[:, off + 2:off + 2 + n], in_=ps[:, c0:c1], func=AF.Copy)
                        for si, (c0, c1, q, slot) in enumerate(segs):
                            st = states[(slot, l)]
                            off = c0 + 2 * si
                            n = c1 - c0
                            S.op("dve", "tensor_copy", out=st.cc[:, f, :], in_=up[:, off + n:off + n + 2])
                            S.op("dve", "scalar_tensor_tensor", out=a[:, c0:c1], in0=up[:, off + 1:off + 1 + n].also(upp),
                                 scalar=w1, in1=a[:, c0:c1], op0=ALU.mult, op1=ALU.add)
                            S.op("dve", "scalar_tensor_tensor", out=a[:, c0:c1], in0=up[:, off:off + n].also(upp), scalar=w0,
                                 in1=a[:, c0:c1], op0=ALU.mult, op1=ALU.add)
                    for it in pending:
                        back(it)
                    pending = [(j, blk, ncols, accs)]
        for it in pending:
            back(it)
        NLATE = 1
        opened = []
        for m in range(2):
            wb = wdn_s[m % 3]
            for (blk, ncols, segs) in blocks:
                ps = fm_bank(banks)
                for kt in range(NKT - NLATE):
                    S.op("pe", "matmul", signal=False, out=ps[:, 0:ncols], lhsT=wb[:, kt, :],
                         rhs=actT[kt][blk][:, 0:ncols], start=(kt == 0), stop=False)
                opened.append((m, wb, ps, blk, ncols, segs))
        for (m, wb, ps, blk, ncols, segs) in opened:
            for kt in range(NKT - NLATE, NKT):
                S.op("pe", "matmul", signal=(kt == NKT - 1), out=ps[:, 0:ncols], lhsT=wb[:, kt, :],
                     rhs=actT[kt][blk][:, 0:ncols], start=False, stop=(kt == NKT - 1))
            for (c0, c1, q, slot) in segs:
                S.op("dve", "scalar_tensor_tensor", out=xT[m][blk][:, c0:c1], in0=ps[:, c0:c1],
                     scalar=mod[l][:, 5 * 8 + m, q:q + 1], in1=xT[m][blk][:, c0:c1], op0=ALU.mult, op1=ALU.add)
        for mm in (3, 4):
            S.dma("pool", wdn_s[mm % 3], wdv[:, :, mm * 128:(mm + 1) * 128])
        for m in range(2, KC):
            wb = wdn_s[m % 3]
            if m >= 3 and m + 2 < KC:
                S.dma("pool", wdn_s[(m + 2) % 3], wdv[:, :, (m + 2) * 128:(m + 3) * 128])
            for (blk, ncols, segs) in blocks:
                ps = fm_bank(banks)
                for kt in range(NKT):
                    S.op("pe", "matmul", signal=(kt == NKT - 1), out=ps[:, 0:ncols], lhsT=wb[:, kt, :],
                         rhs=actT[kt][blk][:, 0:ncols], start=(kt == 0), stop=(kt == NKT - 1))
                for (c0, c1, q, slot) in segs:
                    S.op("dve", "scalar_tensor_tensor", out=xT[m][blk][:, c0:c1], in0=ps[:, c0:c1],
                         scalar=mod[l][:, 5 * 8 + m, q:q + 1], in1=xT[m][blk][:, c0:c1], op0=ALU.mult, op1=ALU.add)

    def init_prompt_states():
        for l in range(depth):
            st = states[(0, l)]
            for b in (st.hist, st.Dm, st.Cbf, st.Eprev, st.cc):
                S.op("dve", "memset", ap=b, constant=0.0)
            S.op("dve", "memset", ap=st.glast, constant=1.0)
            S.op("dve", "memset", ap=st.Bprev, constant=0.0)
            S.op("dve", "memset", ap=st.mprev, constant=0.0)

    def init_sample_states():
        for si in range(2):
            slot = 1 + si
            for l in range(depth):
                st = states[(slot, l)]
                S.dma("sp", st.hist, sp_hist[:, l, si])
                S.dma("sp", st.Dm, sp_C[:, l, si])
                S.dma("sp", st.mprev, sp_mrow[l, si])
                S.dma("sp", st.Eprev, sp_mbc[:, l, si])
                S.dma("sp", st.cc, sp_conv[:, l, si])
                S.op("dve", "tensor_copy", out=st.Cbf, in_=st.Dm)
                S.op("dve", "tensor_scalar", out=st.Eprev, in0=st.Eprev, scalar1=-1.0, scalar2=None, op0=ALU.mult)
                S.op("dve", "memset", ap=st.glast, constant=1.0)
                S.op("dve", "memset", ap=st.Bprev, constant=0.0)

    def store_states(slot, q):
        for l in range(depth):
            st = states[(slot, l)]
            S.dma("sp", o_pool[:, l, q], st.hist)
            S.dma("sp", o_C[:, l, q], st.Dm)
            S.dma("sp", o_m[l, q], st.mprev)
            S.dma("sp", o_conv[:, l, q], st.cc)

    def final_norm_store(blk, ncols, dst_ap, par):
        y = yT[par]
        outs = [Buf_view(y, kc) for kc in range(KC)]
        norm(blk, ncols, [(0, ncols, 0)], lambda kc, q: pv[:, 2 * PVL + kc:2 * PVL + kc + 1],
             lambda kc, q: 0.0, outs)
        S.dma("sp", dst_ap, y[:, :, 0:ncols])

    def Buf_view(y, kc):
        return y[:, kc, :]

    groups = []
    for s in range(2):
        for g in range(T // GT):
            groups.append((s, g))
    sample_segs = [(0, TS, 2, 1), (TS, 2 * TS, 3, 2)]
    sample_tiles = [(0, TS, 2, 1), (TS, TS, 3, 2)]

    for gi, (s, g) in enumerate(groups):
        last = gi == len(groups) - 1
        t0 = g * GT
        S.dma("sp", _multi([xT[kc][b] for kc in range(KC) for b in range(NBLK)], xT_t[:, :, 0:GT]), xp[s][:, :, t0:t0 + GT])
        if g == 0:
            init_prompt_states()
        blocks = [(b, NB, [(0, NB, s, 0)]) for b in range(NBLK)]
        if last:
            S.dma("sp", _multi([xT[kc][SBLK] for kc in range(KC)], xT_t[:, :, GT:GT + SB]), xs)
            init_sample_states()
            blocks.append((SBLK, SB, sample_segs))

        def tiles_of(b):
            return sample_tiles if b == SBLK else [(c, 128, s, 0) for c in range(0, NB, 128)]
        for l in range(depth):
            load_w1(l)
            for (blk, ncols, segs) in blocks:
                phase1_block(l, blk, ncols, segs, tiles_of(blk), first_block=(g == 0 and blk == 0))
            S.barrier()
            phase2(l, blocks)
            S.barrier()
        for bi, (blk, ncols, segs) in enumerate(blocks):
            if blk == SBLK:
                dst = ys
            else:
                dst = yp[s][:, :, t0 + blk * NB:t0 + (blk + 1) * NB]
            final_norm_store(blk, ncols, dst, bi % 2)
        if g == T // GT - 1:
            store_states(0, s)
        if last:
            store_states(1, 2)
            store_states(2, 3)
        S.barrier()

    block = es.enter_context(nc.Block())
    S.replay(block)
    es.close()
    return nc


class _MultiView(View):
    pass


def _multi(bufs, ap):
    v = _MultiView(bufs[0], ap)
    v.bufs = bufs
    return v


_orig_dma = Sched.dma


def _dma(self, q, out, in_):
    if isinstance(out, _MultiView):
        extra = out.bufs[1:]
        self._deps(q, [], extra)
        _orig_dma(self, q, View(out.bufs[0], out.ap), in_)
        tick = out.bufs[0].last_write
        for b in extra:
            b.last_write = tick
            b.readers = {}
    else:
        _orig_dma(self, q, out, in_)


Sched.dma = _dma


def _cols(v):
    v = np.asarray(v, np.float32)
    return np.ascontiguousarray(v.reshape(-1, 128).T)


def _consts():
    ident = np.eye(128, dtype=np.float32)
    tri = np.triu(np.ones((128, 128), np.float32))
    mask = np.tile(tri, (1, 4))
    sel = np.zeros((128, 256), np.float32)
    sel[127, 0:128] = 1.0
    sel[63, 128:256] = 1.0
    rc = np.zeros((128, 64), np.float32)
    for g, w in enumerate(WINDOWS):
        for t in range(16):
            rc[:, 16 * g + t] = 1.0 / min(w, t + 1)
    return ident, mask, sel, rc


_PROG_CACHE = {}


def kernel(x_prompt, x_sample, state_pool, state_mlstm_C, state_mlstm_n, state_mlstm_m, state_conv,
           c_prompt, c_sample, w_ada, b_ada, g_norm1, w_in, b_gate, w_pool, s_pool, g_head, w_out,
           g_norm2, w_up, w_conv, b_conv, w_down, g_final, _cfg=None):
    f32 = np.float32
    x_prompt = np.asarray(x_prompt, f32)
    x_sample = np.asarray(x_sample, f32)
    B, T, _ = x_prompt.shape
    BS, TS, _ = x_sample.shape
    depth = w_in.shape[0]
    ncores = B // 2
    assert BS == B
    GT = min(1024, T)
    NB = min(512, GT)
    if _cfg is not None:
        GT, NB = _cfg
    key = (T, GT, NB, TS, depth)
    if key not in _PROG_CACHE:
        _PROG_CACHE[key] = build_program(T, GT, NB, TS, depth)
    nc = _PROG_CACHE[key]

    ident, mask, sel, rc = _consts()
    pvl = []
    for l in range(depth):
        pvl += [_cols(b_ada[l]), _cols(g_norm1[l]), _cols(g_norm2[l]), _cols(s_pool[l]), _cols(g_head[l]),
                np.ascontiguousarray(np.asarray(w_conv[l], f32).reshape(3, NFT, 128).transpose(2, 0, 1).reshape(128, 3 * NFT)),
                _cols(b_conv[l])]
    pvl.append(_cols(g_final))
    pv = np.ascontiguousarray(np.concatenate(pvl, axis=1))
    assert pv.shape == (128, NPV)
    bgv = np.asarray(b_gate, f32)
    bg = np.zeros((4, depth * 2), f32)
    for l in range(depth):
        bg[:, 2 * l] = bgv[l, 0:4]
        bg[:, 2 * l + 1] = bgv[l, 4:8]
    wts = dict(w_ada=np.asarray(w_ada, f32), w_in=np.asarray(w_in, f32), w_pool=np.asarray(w_pool, f32),
               w_out=np.asarray(w_out, f32), w_up=np.asarray(w_up, f32), w_down=np.asarray(w_down, f32))
    state_pool = np.asarray(state_pool, f32)
    state_mlstm_C = np.asarray(state_mlstm_C, f32)
    state_mlstm_n = np.asarray(state_mlstm_n, f32)
    state_mlstm_m = np.asarray(state_mlstm_m, f32)
    state_conv = np.asarray(state_conv, f32)
    c_prompt = np.asarray(c_prompt, f32)
    c_sample = np.asarray(c_sample, f32)

    in_maps = []
    for c in range(ncores):
        bs = slice(2 * c, 2 * c + 2)
        xp = np.ascontiguousarray(x_prompt[bs].reshape(2, T, KC, 128).transpose(0, 3, 2, 1))
        xs = np.ascontiguousarray(x_sample[bs].reshape(2 * TS, KC, 128).transpose(2, 1, 0))
        cc = np.concatenate([c_prompt[bs], c_sample[bs]], 0)
        cT = np.ascontiguousarray(cc.reshape(4, KC, 128).transpose(2, 1, 0))
        hist = np.ascontiguousarray(state_pool[:, bs].reshape(depth, 2, 15, 4, 128).transpose(4, 0, 1, 3, 2))
        Ct = state_mlstm_C[:, bs].transpose(4, 0, 1, 2, 3)
        nt = state_mlstm_n[:, bs].transpose(3, 0, 1, 2)[..., None]
        spC = np.ascontiguousarray(np.concatenate([Ct, nt], axis=-1))
        mrow = np.ascontiguousarray(state_mlstm_m[:, bs][..., None])
        mbc = np.ascontiguousarray(np.broadcast_to(state_mlstm_m[:, bs][None], (128, depth, 2, NH)))
        conv = np.ascontiguousarray(state_conv[:, bs].reshape(depth, 2, 2, NFT, 128).transpose(4, 0, 1, 3, 2))
        m = dict(xp=xp, xs=xs, cT=cT, pv=pv, bg=bg, ident=ident, mask=mask, sel=sel, rc=rc,
                 sp_hist=hist, sp_C=spC, sp_mrow=mrow, sp_mbc=mbc, sp_conv=conv)
        m.update(wts)
        in_maps.append(m)

    res = run_bass_kernel_spmd(nc, in_maps, core_ids=list(range(ncores)))
    R = res.results

    y_prompt = np.zeros((B, T, D), f32)
    y_sample = np.zeros((BS, TS, D), f32)
    outs = {}
    for nm, nb in (("p", B), ("s", BS)):
        outs["pool" + nm] = np.zeros((depth, nb, 15, DP), f32)
        outs["C" + nm] = np.zeros((depth, nb, NH, HD, HD), f32)
        outs["n" + nm] = np.zeros((depth, nb, NH, HD), f32)
        outs["m" + nm] = np.zeros((depth, nb, NH), f32)
        outs["conv" + nm] = np.zeros((depth, nb, 2, 2 * DFF), f32)
    for c in range(ncores):
        r = R[c]
        ypc = np.asarray(r["yp"])
        y_prompt[2 * c:2 * c + 2] = ypc.transpose(0, 3, 2, 1).reshape(2, T, D)
        ysc = np.asarray(r["ys"])
        y_sample[2 * c:2 * c + 2] = ysc.transpose(2, 1, 0).reshape(2, TS, D)
        op_ = np.asarray(r["o_pool"])
        oC = np.asarray(r["o_C"])
        om = np.asarray(r["o_m"])[..., 0]
        oc = np.asarray(r["o_conv"])
        for q in range(4):
            nm = "p" if q < 2 else "s"
            b = 2 * c + (q % 2)
            outs["pool" + nm][:, b] = op_[:, :, q].transpose(1, 3, 2, 0).reshape(depth, 15, DP)
            outs["C" + nm][:, b] = oC[:, :, q, :, 0:128].transpose(1, 2, 3, 0)
            outs["n" + nm][:, b] = oC[:, :, q, :, 128].transpose(1, 2, 0)
            outs["m" + nm][:, b] = om[:, q, :]
            outs["conv" + nm][:, b] = oc[:, :, q].transpose(1, 3, 2, 0).reshape(depth, 2, 2 * DFF)
    return (y_prompt, y_sample,
            outs["poolp"], outs["Cp"], outs["np"], outs["mp"], outs["convp"],
            outs["pools"], outs["Cs"], outs["ns"], outs["ms"], outs["convs"])
```
